# Optimizing a Trainium2 kernel written in Bass

```python
import jax, jax.numpy as jnp
from jax import lax
import numpy as np

D_MODEL = 1024
BATCH = 8
SEQ = 2048
DEPTH = 1
DEC_BATCH = 128
DEC_SEQ = 1
PAST_LEN = 16384
PAGE_SIZE = 128

N_META = 16
D_CONV = D_MODEL
CONV_A_W = 3
D_RNN = D_MODEL
N_RNN_HEADS = 16
RNN_HEAD_DIM = D_RNN // N_RNN_HEADS
CONV_B_W = 4
RG_C = 8.0
D_FF = 2816
D_IN = 3 * D_CONV + 2 * D_RNN + 2 * D_MODEL
EPS = 1e-6

kernel_name = "hybrid_shortconv_rglru_macaron_step"


def _rmsnorm(x, g):
    xf = x.astype(jnp.float32)
    y = xf * lax.rsqrt(jnp.mean(xf * xf, axis=-1, keepdims=True) + EPS) * g.astype(jnp.float32)
    return y.astype(x.dtype)


def _swiglu(x, w_gate, w_up, w_down):
    return (jax.nn.silu(x @ w_gate) * (x @ w_up)) @ w_down


def _causal_dwconv(x, buf, w):
    T = x.shape[1]
    K = w.shape[0]
    xp = jnp.concatenate([buf.astype(x.dtype), x], axis=1)
    y = xp[:, 0:T] * w[0]
    for k in range(1, K):
        y = y + xp[:, k:k + T] * w[k]
    return y, xp[:, xp.shape[1] - (K - 1):]


def _rglru(x, h0, w_r, b_r, w_i, b_i, lam):
    Bsz, T, _ = x.shape
    xh = x.reshape(Bsz, T, N_RNN_HEADS, RNN_HEAD_DIM)
    r = jax.nn.sigmoid((jnp.einsum('bthi,hij->bthj', xh, w_r).reshape(Bsz, T, D_RNN) + b_r).astype(jnp.float32))
    i = jax.nn.sigmoid((jnp.einsum('bthi,hij->bthj', xh, w_i).reshape(Bsz, T, D_RNN) + b_i).astype(jnp.float32))
    log_a = -RG_C * r * jax.nn.softplus(-lam.astype(jnp.float32))
    a = jnp.exp(log_a)
    u = jnp.sqrt(-jnp.expm1(2.0 * log_a)) * (i * x.astype(jnp.float32))

    def step(h, au):
        a_t, u_t = au
        h = a_t * h + u_t
        return h, h

    h_last, hs = lax.scan(step, h0, (jnp.swapaxes(a, 0, 1), jnp.swapaxes(u, 0, 1)))
    return jnp.swapaxes(hs, 0, 1).astype(x.dtype), h_last


def _mixer(h, buf_a, buf_b, h0, p):
    u = h @ p['w_in']
    idx = [D_CONV, 2 * D_CONV, 3 * D_CONV, 3 * D_CONV + D_RNN,
           3 * D_CONV + 2 * D_RNN, 3 * D_CONV + 2 * D_RNN + D_MODEL]
    a_b, a_c, a_x, b_x, b_gate, g_a, g_b = jnp.split(u, idx, axis=-1)
    conv_a, new_buf_a = _causal_dwconv(a_c * a_x, buf_a, p['conv_a_w'])
    y_a = (a_b * conv_a) @ p['w_out_a']
    conv_b, new_buf_b = _causal_dwconv(b_x, buf_b, p['conv_b_w'])
    conv_b = conv_b + p['conv_b_b']
    rg, h_last = _rglru(conv_b, h0, p['w_rg_r'], p['b_rg_r'], p['w_rg_i'], p['b_rg_i'], p['rg_lambda'])
    y_b = (jax.nn.gelu(b_gate, approximate=True) * rg) @ p['w_out_b']
    merged = jax.nn.sigmoid(g_a) * y_a + jax.nn.sigmoid(g_b) * y_b
    return merged @ p['w_o'], new_buf_a, new_buf_b, h_last


def _layer(x, buf_a, buf_b, h0, p):
    x = x + 0.5 * _rmsnorm(_swiglu(_rmsnorm(x, p['g_ffn1_pre']), p['w_ffn1_gate'], p['w_ffn1_up'], p['w_ffn1_down']), p['g_ffn1_post'])
    m, nba, nbb, hl = _mixer(_rmsnorm(x, p['g_mix_pre']), buf_a, buf_b, h0, p)
    x = x + _rmsnorm(m, p['g_mix_post'])
    x = x + 0.5 * _rmsnorm(_swiglu(_rmsnorm(x, p['g_ffn2_pre']), p['w_ffn2_gate'], p['w_ffn2_up'], p['w_ffn2_down']), p['g_ffn2_post'])
    return x, nba, nbb, hl


def setup_inputs(seed: int = 0) -> dict:
    key = jax.random.key(seed)
    ks = iter(jax.random.split(key, 40))
    f32 = jnp.float32

    def nrm(shape, scale):
        return jax.random.normal(next(ks), shape, f32) * scale

    def gain(shape):
        return 1.0 + 0.05 * jax.random.normal(next(ks), shape, f32)

    L = DEPTH
    ac = jax.random.uniform(next(ks), (L, D_RNN), f32, 0.9, 0.999)
    a0 = ac ** (1.0 / RG_C)
    lam = jnp.log(a0) - jnp.log1p(-a0)
    return {
        'x_prompt': nrm((BATCH, SEQ, D_MODEL), 1.0),
        'x_sample': nrm((DEC_BATCH, DEC_SEQ, D_MODEL), 1.0),
        'state_conv_a': nrm((L, DEC_BATCH, CONV_A_W - 1, D_CONV), 1.0),
        'state_conv_b': nrm((L, DEC_BATCH, CONV_B_W - 1, D_RNN), 1.0),
        'state_rglru': nrm((L, DEC_BATCH, D_RNN), 0.5),
        'meta_tokens': nrm((N_META, D_MODEL), 1.0),
        'g_ffn1_pre': gain((L, D_MODEL)),
        'g_ffn1_post': gain((L, D_MODEL)),
        'w_ffn1_gate': nrm((L, D_MODEL, D_FF), D_MODEL ** -0.5),
        'w_ffn1_up': nrm((L, D_MODEL, D_FF), D_MODEL ** -0.5),
        'w_ffn1_down': nrm((L, D_FF, D_MODEL), D_FF ** -0.5),
        'g_mix_pre': gain((L, D_MODEL)),
        'g_mix_post': gain((L, D_MODEL)),
        'w_in': nrm((L, D_MODEL, D_IN), D_MODEL ** -0.5),
        'conv_a_w': nrm((L, CONV_A_W, D_CONV), CONV_A_W ** -0.5),
        'w_out_a': nrm((L, D_CONV, D_MODEL), D_CONV ** -0.5),
        'conv_b_w': nrm((L, CONV_B_W, D_RNN), CONV_B_W ** -0.5),
        'conv_b_b': nrm((L, D_RNN), 0.02),
        'w_rg_r': nrm((L, N_RNN_HEADS, RNN_HEAD_DIM, RNN_HEAD_DIM), RNN_HEAD_DIM ** -0.5),
        'b_rg_r': nrm((L, D_RNN), 0.02),
        'w_rg_i': nrm((L, N_RNN_HEADS, RNN_HEAD_DIM, RNN_HEAD_DIM), RNN_HEAD_DIM ** -0.5),
        'b_rg_i': nrm((L, D_RNN), 0.02),
        'rg_lambda': lam,
        'w_out_b': nrm((L, D_RNN, D_MODEL), D_RNN ** -0.5),
        'w_o': nrm((L, D_MODEL, D_MODEL), D_MODEL ** -0.5),
        'g_ffn2_pre': gain((L, D_MODEL)),
        'g_ffn2_post': gain((L, D_MODEL)),
        'w_ffn2_gate': nrm((L, D_MODEL, D_FF), D_MODEL ** -0.5),
        'w_ffn2_up': nrm((L, D_MODEL, D_FF), D_MODEL ** -0.5),
        'w_ffn2_down': nrm((L, D_FF, D_MODEL), D_FF ** -0.5),
    }


def reference(x_prompt, x_sample, state_conv_a, state_conv_b, state_rglru, meta_tokens,
              g_ffn1_pre, g_ffn1_post, w_ffn1_gate, w_ffn1_up, w_ffn1_down,
              g_mix_pre, g_mix_post, w_in, conv_a_w, w_out_a, conv_b_w, conv_b_b,
              w_rg_r, b_rg_r, w_rg_i, b_rg_i, rg_lambda, w_out_b, w_o,
              g_ffn2_pre, g_ffn2_post, w_ffn2_gate, w_ffn2_up, w_ffn2_down):
    Bp = x_prompt.shape[0]
    meta = jnp.broadcast_to(meta_tokens.astype(x_prompt.dtype)[None], (Bp, N_META, D_MODEL))
    xp = jnp.concatenate([meta, x_prompt], axis=1)
    xs = x_sample
    pa, pb, ph, sa, sb, sh = [], [], [], [], [], []
    for l in range(DEPTH):
        p = {
            'g_ffn1_pre': g_ffn1_pre[l], 'g_ffn1_post': g_ffn1_post[l],
            'w_ffn1_gate': w_ffn1_gate[l], 'w_ffn1_up': w_ffn1_up[l], 'w_ffn1_down': w_ffn1_down[l],
            'g_mix_pre': g_mix_pre[l], 'g_mix_post': g_mix_post[l], 'w_in': w_in[l],
            'conv_a_w': conv_a_w[l], 'w_out_a': w_out_a[l], 'conv_b_w': conv_b_w[l], 'conv_b_b': conv_b_b[l],
            'w_rg_r': w_rg_r[l], 'b_rg_r': b_rg_r[l], 'w_rg_i': w_rg_i[l], 'b_rg_i': b_rg_i[l],
            'rg_lambda': rg_lambda[l], 'w_out_b': w_out_b[l], 'w_o': w_o[l],
            'g_ffn2_pre': g_ffn2_pre[l], 'g_ffn2_post': g_ffn2_post[l],
            'w_ffn2_gate': w_ffn2_gate[l], 'w_ffn2_up': w_ffn2_up[l], 'w_ffn2_down': w_ffn2_down[l],
        }
        za = jnp.zeros((Bp, CONV_A_W - 1, D_CONV), xp.dtype)
        zb = jnp.zeros((Bp, CONV_B_W - 1, D_RNN), xp.dtype)
        zh = jnp.zeros((Bp, D_RNN), jnp.float32)
        xp, nba, nbb, nh = _layer(xp, za, zb, zh, p)
        pa.append(nba.astype(state_conv_a.dtype))
        pb.append(nbb.astype(state_conv_b.dtype))
        ph.append(nh.astype(state_rglru.dtype))
        xs, nba, nbb, nh = _layer(xs, state_conv_a[l], state_conv_b[l], state_rglru[l].astype(jnp.float32), p)
        sa.append(nba.astype(state_conv_a.dtype))
        sb.append(nbb.astype(state_conv_b.dtype))
        sh.append(nh.astype(state_rglru.dtype))
    y_prompt = xp[:, N_META:]
    y_sample = xs
    return (y_prompt, y_sample, jnp.stack(pa), jnp.stack(pb), jnp.stack(ph), jnp.stack(sa), jnp.stack(sb), jnp.stack(sh))
```

```python
from contextlib import ExitStack

import numpy as np
import concourse.bass as bass
import concourse.mybir as mybir
from concourse.bass_utils import run_bass_kernel_spmd

F32 = mybir.dt.float32
BF16 = mybir.dt.bfloat16
AF = mybir.ActivationFunctionType
ALU = mybir.AluOpType

NCORES = 8
D = 1024
KC = 8
DFF = 2816
HC = 22
NMETA = 16
SEQ = 2048
NSEQ = NMETA + SEQ
NSAMP = 16
NTOK = NSEQ + NSAMP
TILES = [(0, 688, 0), (688, 688, 0), (1376, 688, 16)]
TMAX = 704
EPS = 1e-6
RG_C = 8.0
NSLOT = 5
SLOT_ELEMS = 4096
NPV = 17
NST_IN = 96
NST_OUT = 102

def _unit_list():
    u = []
    for i in range(11):
        u.append(("GU1", i, 4096))
    for i in range(8):
        u.append(("DN1", i, 2816))
    for i in range(8):
        u.append(("MA", i, 4096))
    for i in range(2):
        u.append(("MG", i, 4096))
    for i in range(8):
        u.append(("MO", i, 4096))
    for i in range(2):
        u.append(("WO", i, 4096))
    for i in range(11):
        u.append(("GU2", i, 4096))
    for i in range(8):
        u.append(("DN2", i, 2816))
    return u


UNIT_LIST = _unit_list()
UNIT_OFF = []
_o = 0
for _k, _i, _n in UNIT_LIST:
    UNIT_OFF.append(_o)
    _o += _n
WTOT = _o
NUNITS = len(UNIT_LIST)


def _fm(w, cols):
    kc = w.shape[0] // 128
    sub = w[:, cols]
    return sub.reshape(kc, 128, sub.shape[1]).transpose(1, 0, 2)


def _build_wall(w):
    wall = np.empty((128, WTOT), np.float32)
    for (kind, i, n), off in zip(UNIT_LIST, UNIT_OFF):
        if kind in ("GU1", "GU2"):
            f = kind[-1]
            g = _fm(w[f"w_ffn{f}_gate"], slice(i * 256, (i + 1) * 256)).reshape(128, -1)
            u = _fm(w[f"w_ffn{f}_up"], slice(i * 256, (i + 1) * 256)).reshape(128, -1)
            blk = np.concatenate([g, u], axis=1)
        elif kind in ("DN1", "DN2"):
            f = kind[-1]
            blk = _fm(w[f"w_ffn{f}_down"], slice(i * 128, (i + 1) * 128)).reshape(128, -1)
        elif kind == "MA":
            parts = [_fm(w["w_in"], slice(s * 1024 + i * 128, s * 1024 + (i + 1) * 128)).reshape(128, -1)
                     for s in range(4)]
            blk = np.concatenate(parts, axis=1)
        elif kind == "MG":
            blk = _fm(w["w_in"], slice(4096 + i * 512, 4096 + (i + 1) * 512)).reshape(128, -1)
        elif kind == "MO":
            parts = [
                _fm(w["w_out_a"], slice(i * 128, (i + 1) * 128)).reshape(128, -1),
                _fm(w["w_out_b"], slice(i * 128, (i + 1) * 128)).reshape(128, -1),
                _fm(w["w_in"], slice(5120 + i * 128, 5120 + (i + 1) * 128)).reshape(128, -1),
                _fm(w["w_in"], slice(6144 + i * 128, 6144 + (i + 1) * 128)).reshape(128, -1),
            ]
            blk = np.concatenate(parts, axis=1)
        elif kind == "WO":
            blk = _fm(w["w_o"], slice(i * 512, (i + 1) * 512)).reshape(128, -1)
        assert blk.shape[1] == n, (kind, blk.shape, n)
        wall[:, off:off + n] = blk
    return wall


SYNC_LAT = 0.20
ACT_TBL_COST = 1.3
WINDOW = 64
READY_EPS = 0.0
PRI_ON = 0
LEAD = 2
PRI_PENALTY = 0.0


class Op:
    __slots__ = ("idx", "eng", "fn", "deps", "dur", "kind", "semkey", "tbl", "issue",
                 "users", "nwait", "ready", "start", "end", "pos", "count", "done", "pri", "qprev", "qusers")

    def __init__(self, idx, eng, fn, deps, dur, kind, semkey=None, tbl=None, issue=0.0):
        self.idx = idx
        self.eng = eng
        self.fn = fn
        self.deps = deps
        self.dur = dur
        self.kind = kind
        self.semkey = semkey
        self.tbl = tbl
        self.issue = issue
        self.users = []
        self.done = False
        self.pri = 0
        self.qprev = None
        self.qusers = []


class Sched:
    ENGS = ("pe", "act", "dve", "pool", "sp")

    def __init__(self):
        self.ops = []
        self.last_write = {}
        self.readers = {}
        self.last_dma_on_queue = {}

    def _deps(self, reads, writes):
        deps = set()
        for r in reads:
            w = self.last_write.get(r)
            if w is not None:
                deps.add(w)
        for w_ in writes:
            w = self.last_write.get(w_)
            if w is not None:
                deps.add(w)
            deps.update(self.readers.get(w_, ()))
        return deps

    def _commit(self, idx, reads, writes):
        for w in writes:
            self.last_write[w] = idx
            self.readers[w] = []
        for r in reads:
            self.readers.setdefault(r, []).append(idx)

    STREAM_KEYS = frozenset(("xs", "hy", "big", "sqb", "rstd", "sgb", "acx", "bx", "cb", "cbh",
                             "A1", "A2", "B1", "B2", "B3"))

    def _pri(self, reads, writes):
        p = 0
        for k in list(writes) + list(reads):
            if isinstance(k, tuple) and k[0] in self.STREAM_KEYS and k[-1] == 1:
                p = 1
        return p

    def op(self, engname, fn, reads=(), writes=(), dur=0.5, tbl=None):
        deps = self._deps(reads, writes)
        o = Op(len(self.ops), engname, fn, deps, dur, "c", tbl=tbl)
        o.pri = self._pri(reads, writes)
        self.ops.append(o)
        self._commit(o.idx, reads, writes)
        return o.idx

    def dma(self, engname, semkey, fn, reads=(), writes=(), dur=3.0, extra_deps=()):
        deps = self._deps(reads, writes)
        deps.update(extra_deps)
        issue = 0.7 if engname == "pool" else 0.1
        o = Op(len(self.ops), engname, fn, deps, dur, "d", semkey=semkey, issue=issue)
        o.qprev = self.last_dma_on_queue.get(engname)
        self.ops.append(o)
        self.last_dma_on_queue[engname] = o.idx
        self._commit(o.idx, reads, writes)
        return o.idx

    def wait_all(self, engname, events):
        o = Op(len(self.ops), engname, None, set(events), 0.0, "w")
        self.ops.append(o)
        return o.idx

    def finalize(self):
        ops = self.ops
        for o in ops:
            o.nwait = len(o.deps)
            o.ready = 0.0
            for d in o.deps:
                ops[d].users.append(o.idx)
            if o.qprev is not None:
                o.nwait += 1
                ops[o.qprev].qusers.append(o.idx)
        queues = {e: [o.idx for o in ops if o.eng == e] for e in self.ENGS}
        heads = {e: 0 for e in self.ENGS}
        t_free = {e: 0.0 for e in self.ENGS}
        cur_tbl = [None]
        order = {e: [] for e in self.ENGS}
        remaining = len(ops)
        while remaining:
            best = None
            for e in self.ENGS:
                q = queues[e]
                h = heads[e]
                while h < len(q) and ops[q[h]].done:
                    h += 1
                heads[e] = h
                seen = 0
                i = h
                tf = t_free[e]
                ebest = None
                while i < len(q) and seen < WINDOW:
                    o = ops[q[i]]
                    i += 1
                    if o.done:
                        continue
                    seen += 1
                    if o.nwait:
                        continue
                    st = max(o.ready, tf)
                    if e == "act" and o.tbl is not None and cur_tbl[0] is not None and o.tbl != cur_tbl[0]:
                        st += ACT_TBL_COST
                    if st <= tf + READY_EPS:
                        key = (tf, PRI_ON * o.pri, o.idx)
                    else:
                        key = (st, 0, o.idx)
                    if ebest is None or key < ebest[0]:
                        ebest = (key, o)
                if ebest is not None and (best is None or ebest[0] < best[0]):
                    best = ebest
            assert best is not None, "scheduler deadlock"
            o = best[1]
            st = max(o.ready, t_free[o.eng])
            if o.eng == "act" and o.tbl is not None and cur_tbl[0] is not None and o.tbl != cur_tbl[0]:
                st += ACT_TBL_COST
            o.start = st
            if o.kind == "d":
                t_free[o.eng] = st + o.issue
                o.end = st + o.issue + o.dur
            else:
                o.end = st + o.dur
                t_free[o.eng] = o.end
                if o.eng == "act" and o.tbl is not None:
                    cur_tbl[0] = o.tbl
            o.done = True
            order[o.eng].append(o)
            remaining -= 1
            for u in o.qusers:
                uo = ops[u]
                uo.nwait -= 1
                if o.start + o.issue > uo.ready:
                    uo.ready = o.start + o.issue
            for u in o.users:
                uo = ops[u]
                uo.nwait -= 1
                lat = SYNC_LAT if (uo.eng != o.eng or o.kind == "d") else 0.05
                if o.end + lat > uo.ready:
                    uo.ready = o.end + lat
        self.makespan = max(o.end for o in ops)
        for e in self.ENGS:
            pos = 0
            for o in order[e]:
                if o.kind == "c":
                    pos += 1
                    o.pos = pos
        dma_count = {}
        for e in self.ENGS:
            for o in order[e]:
                if o.kind == "d":
                    c = dma_count.get(o.semkey, 0) + 16
                    dma_count[o.semkey] = c
                    o.count = c
        prog = {}
        for e in self.ENGS:
            waited = {}
            lst = []
            for o in order[e]:
                need = {}
                for d in o.deps:
                    do = ops[d]
                    if do.kind == "d":
                        k, v = do.semkey, do.count
                    elif do.kind == "c":
                        if do.eng == "pe" and e == "pe":
                            continue
                        k, v = do.eng, do.pos
                    else:
                        continue
                    if need.get(k, 0) < v:
                        need[k] = v
                waits = []
                for k, v in need.items():
                    if waited.get(k, 0) < v:
                        waits.append((k, v))
                        waited[k] = v
                if o.kind == "c":
                    inc = (e, 1)
                elif o.kind == "d":
                    inc = (o.semkey, 16)
                else:
                    inc = None
                lst.append((waits, o.fn, inc))
            prog[e] = lst
        return prog


NSMAX = 352


class Stm:
    def __init__(self, s, ns, nseq, T):
        self.s = s
        self.c0 = 0 if s == 0 else NSMAX
        self.n = NSMAX if s == 0 else T - NSMAX
        self.c1 = self.c0 + self.n
        self.q0 = min(self.c0, nseq)
        self.q1 = min(self.c1, nseq)
        self.samp = self.c1 > nseq
        self.has_end = (self.q1 == nseq)


def build_program():
    nc = bass.Bass("TRN2", target_bir_lowering=False)
    xin = nc.dram_tensor("xin", [NTOK, D], F32, kind="ExternalInput").ap()
    stin = nc.dram_tensor("stin", [NST_IN, D], F32, kind="ExternalInput").ap()
    par_d = nc.dram_tensor("par", [128, NPV * 8], F32, kind="ExternalInput").ap()
    wall = nc.dram_tensor("wall", [128, WTOT], F32, kind="ExternalInput").ap()
    wrg_d = nc.dram_tensor("wrg", [128, 2048], F32, kind="ExternalInput").ap()
    ident_d = nc.dram_tensor("ident", [128, 128], F32, kind="ExternalInput").ap()
    yout = nc.dram_tensor("yout", [NTOK, D], F32, kind="ExternalOutput").ap()
    stout = nc.dram_tensor("stout", [NST_OUT, D], F32, kind="ExternalOutput").ap()

    S = Sched()
    with ExitStack() as ctx:
        def sb(name, shape, dt):
            return ctx.enter_context(nc.sbuf_tensor(name, shape, dt))

        xsA = sb("xsA", [128, KC, TMAX], F32)
        xsB = sb("xsB", [128, KC, TMAX], F32)
        XS = [xsA, xsB]
        cur = {"xs": xsA, "xb": 0}
        hy = sb("hy", [128, 2, KC * NSMAX], F32)
        big = sb("big", [128, 2, HC * NSMAX], BF16)
        wring = sb("wring", [128, NSLOT, SLOT_ELEMS], BF16)
        wrgs = sb("wrgs", [128, 2048], BF16)
        par = sb("par_sb", [128, NPV * 8], F32)
        dpar = sb("dpar", [128, 6 * 8], F32)
        ident = sb("ident_sb", [128, 128], F32)
        ones = sb("ones", [128, 128], BF16)
        cst = sb("cstcol", [128, 2], F32)
        tblw = sb("tblw", [128, 2], F32)
        NSTG = 2
        stg = sb("stg", [128, NSTG, D], F32)
        xin_st = stg
        yout_st = stg
        sqb = sb("sqb", [128, 2, TMAX], BF16)
        rstd = sb("rstd", [128, TMAX], F32)
        sgb = sb("sgb", [128, 2, TMAX], F32)
        ttmp = sgb
        stT = sb("stT", [128, KC, NST_IN], F32)
        SO = sb("SO", [128, KC, 104], F32)
        CAR = sb("CAR", [128, KC, 8], F32)
        NSET = 2
        assert LEAD <= NSET and NSLOT >= LEAD + 3
        acx = sb("acx", [128, NSET, TMAX + 8], F32)
        bxb = sb("bxb", [128, NSET, TMAX + 8], F32)
        cbb = sb("cbb", [128, 2, TMAX], F32)
        cbh = sb("cbh", [128, 2, TMAX], BF16)
        B1 = sb("B1", [128, NSET, TMAX], F32)
        B2 = sb("B2", [128, NSET, TMAX], F32)
        B3 = sb("B3", [128, NSET, TMAX], F32)
        A2 = sb("A2", [128, 1, TMAX], F32)
        P = ctx.enter_context(nc.psum_tensor("P", [128, 8, 512], F32))

        semnames = ["pe", "act", "dve", "pool", "sp", "c0", "c1", "c2", "stg0", "stg1", "stg2", "stg3", "stg4"] + \
                   [f"w{i}" for i in range(NSLOT)]
        sems = {n: ctx.enter_context(nc.semaphore(n)) for n in semnames}

        HALF = KC * NSMAX // 2
        y32 = [hy[:, s, :].rearrange("p (k t) -> p k t", k=KC) for s in range(2)]
        hbf = [hy[:, s, 0:HALF].bitcast(BF16).rearrange("p (k t) -> p k t", k=KC) for s in range(2)]
        mgbf = [hy[:, s, HALF:2 * HALF].bitcast(BF16).rearrange("p (k t) -> p k t", k=KC) for s in range(2)]
        hid = [big[:, s, :].rearrange("p (k t) -> p k t", k=HC) for s in range(2)]
        m32 = [big[:, s, 0:2 * KC * NSMAX].bitcast(F32).rearrange("p (k t) -> p k t", k=KC) for s in range(2)]

        def R_h(kc, s): return ("hy", kc, s)
        def R_mg(kc, s): return ("hy", 8 + kc, s)
        def R_y32(oc, s): return [("hy", 2 * oc, s), ("hy", 2 * oc + 1, s)]
        def R_hid(hc, s): return ("big", hc, s)
        def R_ya(kc, s): return ("big", kc, s)
        def R_yb(kc, s): return ("big", 8 + kc, s)
        def R_m32(oc, s): return [("big", 2 * oc, s), ("big", 2 * oc + 1, s)]
        def R_xs(kc, s): return ("xs", cur["xb"], kc, s)
        def R_ps(bank): return [("ps", bank)]

        def pcol(vi, kc):
            return par[:, vi * 8 + kc:vi * 8 + kc + 1]

        def dcol(vi, kc):
            return dpar[:, vi * 8 + kc:vi * 8 + kc + 1]

        eps_col = cst[:, 0:1]
        one_col = cst[:, 1:2]

        st = {"ps_next": 0, "held": set(), "unit": 0}

        def alloc_bank():
            for _ in range(16):
                b = st["ps_next"] % 8
                st["ps_next"] += 1
                if b not in st["held"]:
                    return b
            raise RuntimeError("no psum bank")

        def last_pe_idx():
            for o in reversed(S.ops):
                if o.eng == "pe":
                    return o.idx
            return None

        def issue_unit(g, paced=True, dep=None):
            if g >= NUNITS * len(TILES):
                return
            extra = [st["pace"]] if (paced and st.get("pace") is not None) else []
            if dep is not None:
                extra = [dep]
            kind, i, n = UNIT_LIST[g % NUNITS]
            off = UNIT_OFF[g % NUNITS]
            slot = g % NSLOT
            dst = wring[:, slot, 0:n]
            src = wall[:, off:off + n]
            S.dma("pool", f"w{slot}", lambda e, dst=dst, src=src: e.dma_start(out=dst, in_=src),
                  reads=(), writes=[("w", slot)], dur=2.0 + n * 128 * 4 / 300e3, extra_deps=extra)

        def next_unit(kind, idx):
            g = st["unit"]
            k, i, n = UNIT_LIST[g % NUNITS]
            assert (k, i) == (kind, idx), ((k, i), (kind, idx))
            st["unit"] += 1
            issue_unit(g + NSLOT - 1 - LEAD)
            return g % NSLOT

        def wv(slot, off, kc_n, cols):
            return wring[:, slot, off:off + kc_n * cols].rearrange("p (k c) -> p k c", k=kc_n)

        def mm_group(bank, n, lhs_list, rhs_list, reads, start=True, stop=True):
            nk = len(lhs_list)
            out = P[:, bank, 0:n]

            def fn(e):
                ins = None
                for k in range(nk):
                    ins = e.matmul(out, lhs_list[k], rhs_list[k],
                                   start=(start and k == 0), stop=(stop and k == nk - 1))
                return ins
            return S.op("pe", fn, reads=reads, writes=R_ps(bank), dur=nk * (n / 2400.0 + 0.010))

        def fsz(ap):
            n = 1
            for d in ap.shape[1:]:
                n *= d
            return n

        TBL = {AF.Silu: "silu", AF.Gelu_apprx_tanh: "gelu", AF.Exp: "lnexp", AF.Ln: "lnexp"}

        def act(out, in_, func, reads, writes, bias=None, scale=None):
            kw = {}
            if bias is not None:
                kw["bias"] = bias
            if scale is not None:
                kw["scale"] = scale
            return S.op("act", lambda e: e.activation(out=out, in_=in_, func=func, **kw),
                        reads=reads, writes=writes, dur=0.22 + fsz(out) * 0.00083, tbl=TBL.get(func))

        def vcost(eng, out, nsrc):
            n = fsz(out)
            if eng == "pool":
                return 0.15 + n / 420.0
            if nsrc >= 2:
                return 0.12 + n / 960.0
            return 0.10 + n / 1800.0

        def tt(eng, out, in0, in1, op, reads, writes):
            return S.op(eng, lambda e: e.tensor_tensor(out=out, in0=in0, in1=in1, op=op),
                        reads=reads, writes=writes, dur=vcost(eng, out, 2))

        def ts(eng, out, in0, s1, s2, op0, op1, reads, writes):
            if op1 is None:
                return S.op(eng, lambda e: e.tensor_scalar(out=out, in0=in0, scalar1=s1, scalar2=None, op0=op0),
                            reads=reads, writes=writes, dur=vcost(eng, out, 1))
            return S.op(eng, lambda e: e.tensor_scalar(out=out, in0=in0, scalar1=s1, scalar2=s2, op0=op0, op1=op1),
                        reads=reads, writes=writes, dur=vcost(eng, out, 1))

        def stt(eng, out, in0, scalar, in1, op0, op1, reads, writes):
            return S.op(eng, lambda e: e.scalar_tensor_tensor(out=out, in0=in0, scalar=scalar, in1=in1,
                                                              op0=op0, op1=op1),
                        reads=reads, writes=writes, dur=vcost(eng, out, 2))

        def scan(eng, out, d0, d1, init, reads, writes):
            return S.op(eng, lambda e: e.tensor_tensor_scan(out=out, data0=d0, data1=d1, initial=init,
                                                            op0=ALU.mult, op1=ALU.add),
                        reads=reads, writes=writes, dur=vcost(eng, out, 2))

        def cp(eng, out, in_, reads, writes):
            if eng == "act":
                return act(out, in_, AF.Identity, reads, writes)
            return S.op(eng, lambda e: e.tensor_copy(out=out, in_=in_), reads=reads, writes=writes,
                        dur=vcost(eng, out, 2))

        S.dma("sp", "c0", lambda e: e.dma_start(out=par[:, :], in_=par_d[:, :]), writes=["par"])
        S.dma("sp", "c1", lambda e: e.dma_start(out=ident[:, :], in_=ident_d[:, :]), writes=["ident"])
        S.op("dve", lambda e: e.memset(ones[:, :], 1.0 / D), writes=["ones"], dur=0.1)
        S.op("dve", lambda e: e.memset(CAR[:, :, :], 0.0), writes=[("CAR", j) for j in range(KC)], dur=0.1)
        S.op("dve", lambda e: e.memset(cst[:, 0:1], EPS), writes=["cst0"], dur=0.1)
        S.op("dve", lambda e: e.memset(tblw[:, :], 0.0), writes=["tblw"], dur=0.1)
        S.op("dve", lambda e: e.memset(cst[:, 1:2], 1.0), writes=["cst1"], dur=0.1)
        ts("dve", dpar[:, 0:8], par[:, 14 * 8:15 * 8], -1.0, None, ALU.mult, None, ["par"], ["dpar0"])
        ts("dve", dpar[:, 8:16], par[:, 15 * 8:16 * 8], -1.0, None, ALU.mult, None, ["par"], ["dpar1"])
        act(dpar[:, 16:24], par[:, 16 * 8:17 * 8], AF.Exp, ["par"], ["dpar2"], scale=-1.0)
        ts("dve", dpar[:, 16:24], dpar[:, 16:24], 1.0, None, ALU.add, None, ["dpar2"], ["dpar2"])
        act(dpar[:, 16:24], dpar[:, 16:24], AF.Ln, ["dpar2"], ["dpar2"])
        ts("dve", dpar[:, 16:24], dpar[:, 16:24], -RG_C, None, ALU.mult, None, ["dpar2"], ["dpar2"])
        ts("dve", dpar[:, 24:32], par[:, 1 * 8:2 * 8], 0.5, None, ALU.mult, None, ["par"], ["dpar3"])
        ts("dve", dpar[:, 32:40], par[:, 5 * 8:6 * 8], 0.5, None, ALU.mult, None, ["par"], ["dpar4"])
        ts("dve", dpar[:, 40:48], dpar[:, 16:24], 2.0, None, ALU.mult, None, ["dpar2"], ["dpar5"])
        DP_ALL = ["par", "dpar0", "dpar1", "dpar2", "dpar3", "dpar4", "dpar5", "cst0", "cst1"]

        io = {"stg": 0}

        ALT_STG = {2: ("B1", B1), 3: ("B2", B2), 4: ("B3", B3)}

        def stg_view(b):
            if b < NSTG:
                return stg[:, b, :]
            return ALT_STG[b][1][:, :, :].rearrange("p a b -> p (a b)")[:, 0:D]

        def stg_keys(b):
            if b < NSTG:
                return [("stg", b)]
            nm = ALT_STG[b][0]
            return [(nm, q_, s_) for q_ in range(2) for s_ in range(2)]

        def load_rows(src_rows, nr, extra_slots=False):
            nslots = NSTG + (2 if extra_slots else 0)
            b = io["stg"] % nslots
            io["stg"] += 1
            dst = stg_view(b)[0:nr, :]
            S.dma("sp", f"stg{b}", lambda e: e.dma_start(out=dst, in_=src_rows),
                  writes=stg_keys(b), dur=2.0 + nr * 4096 / 300e3)
            return b

        def transpose_in(b, nr, dst_fn, dst_res_fn):
            b0 = alloc_bank()
            b1 = alloc_bank()
            banks = (b0, b1)
            sview = stg_view(b)

            def fn(e):
                ins = None
                for kc in range(KC):
                    out = P[:, banks[kc // 4], (kc % 4) * 128:(kc % 4) * 128 + nr]
                    ins = e.transpose(out, sview[0:nr, kc * 128:(kc + 1) * 128], ident[0:nr, 0:nr])
                return ins
            S.op("pe", fn, reads=stg_keys(b) + ["ident"], writes=R_ps(b0) + R_ps(b1), dur=8 * 0.2)
            for half in range(2):
                src = P[:, banks[half], :].rearrange("p (k c) -> p k c", k=4)[:, :, 0:nr]
                eng = "act" if half == 0 else "dve"
                cp(eng, dst_fn(half), src, reads=R_ps(banks[half]), writes=dst_res_fn(half))

        def store_rows(src_fn, src_res, nr, dst_rows, extra_slots=False):
            b0 = alloc_bank()
            b1 = alloc_bank()
            banks = (b0, b1)
            srcs = [src_fn(kc) for kc in range(KC)]

            def fn(e):
                ins = None
                for kc in range(KC):
                    out = P[0:nr, banks[kc // 4], (kc % 4) * 128:(kc % 4 + 1) * 128]
                    ins = e.transpose(out, srcs[kc], ident[:, :])
                return ins
            S.op("pe", fn, reads=src_res + ["ident"], writes=R_ps(b0) + R_ps(b1), dur=8 * 0.2)
            nslots = NSTG + (3 if extra_slots else 0)
            b = io["stg"] % nslots
            io["stg"] += 1
            sv = stg_view(b)
            for half in range(2):
                eng = "act" if half == 0 else "dve"
                cp(eng, sv[0:nr, half * 512:(half + 1) * 512], P[0:nr, banks[half], :],
                   reads=R_ps(banks[half]), writes=stg_keys(b))
            return S.dma("sp", f"stg{b}", lambda e: e.dma_start(out=dst_rows, in_=sv[0:nr, :]),
                         reads=stg_keys(b), dur=2.0 + nr * 4096 / 300e3)

        def load_states():
            b = load_rows(stin[0:NST_IN, :], NST_IN)
            transpose_in(b, NST_IN,
                         lambda half: stT[:, 4 * half:4 * half + 4, :],
                         lambda half: ["stT"])
            cp("dve", SO[:, :, 0:16], stT[:, :, 16:32], ["stT"], ["SO_a"])
            cp("dve", SO[:, :, 32:64], stT[:, :, 48:80], ["stT"], ["SO_b"])

        def rbuf_std(sm):
            return rstd[:, sm.c0:sm.c1], ("rstd", sm.s)

        def rbuf_alt(sm):
            return B3[:, 0, sm.c0:sm.c1], ("B3", 0, sm.s)

        def rms_finish(bank, sm, rb=None):
            rv, rk = rb if rb is not None else rbuf_std(sm)
            act(rv, P[:, bank, 0:sm.n], AF.Ln, R_ps(bank) + DP_ALL, [rk], bias=EPS)
            act(rv, rv, AF.Exp, [rk], [rk], scale=-0.5)

        def norm_to_h(sm, gvi):
            norm_stats(sm, None)
            norm_apply(sm, gvi, None)

        def norm_stats(sm, rb):
            s = sm.s
            bank = alloc_bank()
            st["held"].add(bank)
            for kc in range(KC):
                qb = kc % 2
                sq = sqb[:, qb, sm.c0:sm.c1]
                act(sq, cur["xs"][:, kc, sm.c0:sm.c1], AF.Square, [R_xs(kc, s)], [("sqb", qb, s)])
                mm_group(bank, sm.n, [ones[:, :]], [sq], reads=[("sqb", qb, s), "ones"],
                         start=(kc == 0), stop=(kc == KC - 1))
            rms_finish(bank, sm, rb)
            st["held"].discard(bank)

        def norm_apply(sm, gvi, rb):
            s = sm.s
            rv, rk = rb if rb is not None else rbuf_std(sm)
            for kc in range(KC):
                stt("dve", hbf[s][:, kc, 0:sm.n], cur["xs"][:, kc, sm.c0:sm.c1], pcol(gvi, kc), rv,
                    ALU.mult, ALU.mult, [R_xs(kc, s), rk] + DP_ALL, [R_h(kc, s)])

        def evac_with_stats(bY, bS, sm, oc, dst32, dst_res, gain_col, first, last):
            s = sm.s
            qb = oc % 2
            sq = sqb[:, qb, sm.c0:sm.c1]
            act(dst32[s][:, oc, 0:sm.n], P[:, bY, 0:sm.n], AF.Identity, R_ps(bY) + DP_ALL, dst_res(oc, s),
                scale=gain_col)
            act(sq, P[:, bY, 0:sm.n], AF.Square, R_ps(bY), [("sqb", qb, s)])

            def stats():
                mm_group(bS, sm.n, [ones[:, :]], [sq], reads=[("sqb", qb, s), "ones"], start=first, stop=last)
            return stats

        def post_norm_residual(sm, bS, src32, src_res):
            s = sm.s
            rms_finish(bS, sm)
            st["held"].discard(bS)
            for oc in range(KC):
                tb = oc % 2
                tv = ttmp[:, tb, sm.c0:sm.c1]
                tt("dve", tv, src32[s][:, oc, 0:sm.n], rstd[:, sm.c0:sm.c1], ALU.mult,
                   src_res(oc, s) + [("rstd", s)], [("sgb", tb, s)])
                tt("dve", cur["xs"][:, oc, sm.c0:sm.c1], cur["xs"][:, oc, sm.c0:sm.c1], tv, ALU.add,
                   [R_xs(oc, s), ("sgb", tb, s)], [R_xs(oc, s)])

        def run_units(sms, kind, U, work, finish=None, s1_first=False):
            slots = {}

            def do_s1(it):
                u1 = it - LEAD
                if 0 <= u1 < U:
                    work(u1, slots[u1], sms[1])
                    if u1 == U - 1 and finish is not None:
                        finish(sms[1])

            def do_s0(it):
                if it < U:
                    slots[it] = next_unit(kind, it)
                    work(it, slots[it], sms[0])
                    st["pace"] = last_pe_idx()
                    if it == U - 1 and finish is not None:
                        finish(sms[0])

            for it in range(U + LEAD):
                if s1_first:
                    do_s1(it)
                    do_s0(it)
                else:
                    do_s0(it)
                    do_s1(it)

        def ffn(sms, which, next_gvi):
            ghalf = 3 if which == 1 else 4
            def p1(u, slot_w, sm):
                s = sm.s
                wg = wv(slot_w, 0, KC, 256)
                wu = wv(slot_w, 2048, KC, 256)
                rhs = [hbf[s][:, k, 0:sm.n] for k in range(KC)]
                rd = [R_h(k, s) for k in range(KC)] + [("w", slot_w)]
                for bb in range(2):
                    hc = 2 * u + bb
                    bG = alloc_bank()
                    mm_group(bG, sm.n, [wg[:, k, bb * 128:(bb + 1) * 128] for k in range(KC)], rhs, rd)
                    bU = alloc_bank()
                    mm_group(bU, sm.n, [wu[:, k, bb * 128:(bb + 1) * 128] for k in range(KC)], rhs, rd)
                    gb = hc % 2
                    sv = sgb[:, gb, sm.c0:sm.c1]
                    act(sv, P[:, bG, 0:sm.n], AF.Silu, R_ps(bG), [("sgb", gb, s)])
                    tt("dve", hid[s][:, hc, 0:sm.n], P[:, bU, 0:sm.n], sv, ALU.mult,
                       R_ps(bU) + [("sgb", gb, s)], [R_hid(hc, s)])
            run_units(sms, f"GU{which}", 11, p1)
            act(tblw[:, 0:1], tblw[:, 1:2], AF.Exp,
                ["tblw"] + [("sgb", gb_, s_) for gb_ in range(2) for s_ in range(2)], ["tblw"])
            bS = {}
            pending = {}

            def p2(oc, slot_w, sm):
                s = sm.s
                if s not in bS:
                    bS[s] = alloc_bank()
                    st["held"].add(bS[s])
                    pending[s] = None
                wd = wv(slot_w, 0, HC, 128)
                bY = alloc_bank()
                mm_group(bY, sm.n, [wd[:, k, :] for k in range(HC)],
                         [hid[s][:, k, 0:sm.n] for k in range(HC)],
                         [R_hid(k, s) for k in range(HC)] + [("w", slot_w)])
                if pending[s] is not None:
                    pending[s]()
                pending[s] = evac_with_stats(bY, bS[s], sm, oc, y32, R_y32, dcol(ghalf, oc),
                                             oc == 0, oc == KC - 1)

            def f2(sm):
                pending[sm.s]()
                post_norm_residual(sm, bS[sm.s], y32, R_y32)
                if next_gvi is not None:
                    norm_to_h(sm, next_gvi)
            run_units(sms, f"DN{which}", KC, p2, f2)

        def mixer(ti, sms, nseq, T, next_gvi):
            last_tile = (ti == len(TILES) - 1)

            def sigmoid_act(Bx, nm, q, sm, bank, bias_col):
                bv = Bx[:, q, sm.c0:sm.c1]
                key = (nm, q, sm.s)
                if bias_col is None:
                    act(bv, P[:, bank, 0:sm.n], AF.Exp, R_ps(bank), [key], scale=-1.0)
                else:
                    act(bv, P[:, bank, 0:sm.n], AF.Exp, R_ps(bank) + DP_ALL, [key], scale=-1.0, bias=bias_col)
                act(bv, bv, AF.Ln, [key] + DP_ALL, [key], bias=1.0)
                act(bv, bv, AF.Exp, [key], [key], scale=-1.0)

            def stage1(j, sm, slot_w):
                s = sm.s
                q = j % NSET
                q3 = j % 2
                c0, c1, n = sm.c0, sm.c1, sm.n
                q0, q1 = sm.q0, sm.q1
                w4 = wring[:, slot_w, 0:4096].rearrange("p (s k c) -> p s k c", s=4, k=KC)
                rhs = [hbf[s][:, k, 0:n] for k in range(KC)]
                rd = [R_h(k, s) for k in range(KC)] + [("w", slot_w)]

                def grp(sidx):
                    bk = alloc_bank()
                    mm_group(bk, n, [w4[:, sidx, k, :] for k in range(KC)], rhs, rd)
                    return bk
                b_bx = grp(3)
                b_ac = grp(1)
                b_ax = grp(2)
                b_ab = grp(0)
                if s == 0:
                    cp("dve", acx[:, q, 0:2], CAR[:, j, 0:2], [("CAR", j)], [("acx_h", q)])
                    cp("dve", bxb[:, q, 0:3], CAR[:, j, 2:5], [("CAR", j)], [("bx_h", q)])
                    halo_a = [("acx_h", q)]
                    halo_b = [("bx_h", q)]
                else:
                    halo_a = [("acx", q, 0)]
                    halo_b = [("bx", q, 0)]
                act(bxb[:, q, 3 + c0:3 + c1], P[:, b_bx, 0:n], AF.Identity, R_ps(b_bx), [("bx", q, s)])
                act(acx[:, q, 2 + c0:2 + c1], P[:, b_ac, 0:n], AF.Identity, R_ps(b_ac), [("acx", q, s)])
                cbk = ("cb", q3, s)
                ts("dve", cbb[:, q3, q0:q1], bxb[:, q, q0:q1], pcol(9, j), pcol(13, j), ALU.mult, ALU.add,
                   [("bx", q, s)] + halo_b + DP_ALL, [cbk])
                tt("dve", acx[:, q, 2 + c0:2 + c1], P[:, b_ax, 0:n], acx[:, q, 2 + c0:2 + c1], ALU.mult,
                   R_ps(b_ax) + [("acx", q, s)], [("acx", q, s)])
                for k in (1, 2, 3):
                    stt("dve", cbb[:, q3, q0:q1], bxb[:, q, k + q0:k + q1], pcol(9 + k, j), cbb[:, q3, q0:q1],
                        ALU.mult, ALU.add, [("bx", q, s), cbk] + halo_b + DP_ALL, [cbk])
                if sm.samp:
                    ts("dve", cbb[:, q3, nseq:T], stT[:, j, 32:48], pcol(9, j), pcol(13, j), ALU.mult, ALU.add,
                       ["stT", cbk] + DP_ALL, [cbk])
                    stt("dve", cbb[:, q3, nseq:T], stT[:, j, 48:64], pcol(10, j), cbb[:, q3, nseq:T],
                        ALU.mult, ALU.add, ["stT", cbk] + DP_ALL, [cbk])
                    stt("dve", cbb[:, q3, nseq:T], stT[:, j, 64:80], pcol(11, j), cbb[:, q3, nseq:T],
                        ALU.mult, ALU.add, ["stT", cbk] + DP_ALL, [cbk])
                    stt("dve", cbb[:, q3, nseq:T], bxb[:, q, 3 + nseq:3 + T], pcol(12, j), cbb[:, q3, nseq:T],
                        ALU.mult, ALU.add, [("bx", q, s), cbk] + DP_ALL, [cbk])
                act(cbh[:, q, c0:c1], cbb[:, q3, c0:c1], AF.Identity, [cbk], [("cbh", q, s)])
                a2k = ("A2", 0, s)
                ts("dve", A2[:, 0, q0:q1], acx[:, q, q0:q1], pcol(6, j), None, ALU.mult, None,
                   [("acx", q, s)] + halo_a + DP_ALL, [a2k])
                stt("dve", A2[:, 0, q0:q1], acx[:, q, 1 + q0:1 + q1], pcol(7, j), A2[:, 0, q0:q1], ALU.mult, ALU.add,
                    [("acx", q, s), a2k] + halo_a + DP_ALL, [a2k])
                stt("dve", A2[:, 0, q0:q1], acx[:, q, 2 + q0:2 + q1], pcol(8, j), A2[:, 0, q0:q1], ALU.mult, ALU.add,
                    [("acx", q, s), a2k] + DP_ALL, [a2k])
                if sm.samp:
                    ts("dve", A2[:, 0, nseq:T], stT[:, j, 0:16], pcol(6, j), None, ALU.mult, None,
                       ["stT", a2k] + DP_ALL, [a2k])
                    stt("dve", A2[:, 0, nseq:T], stT[:, j, 16:32], pcol(7, j), A2[:, 0, nseq:T], ALU.mult, ALU.add,
                        ["stT", a2k] + DP_ALL, [a2k])
                    stt("dve", A2[:, 0, nseq:T], acx[:, q, 2 + nseq:2 + T], pcol(8, j), A2[:, 0, nseq:T],
                        ALU.mult, ALU.add, [("acx", q, s), a2k] + DP_ALL, [a2k])
                tt("dve", hid[s][:, j, 0:n], P[:, b_ab, 0:n], A2[:, 0, c0:c1], ALU.mult,
                   R_ps(b_ab) + [a2k], [R_ya(j, s)])
                if sm.has_end:
                    if not last_tile:
                        cp("dve", CAR[:, j, 0:2], acx[:, q, nseq:nseq + 2], [("acx", q, s)], [("CAR", j)])
                        cp("dve", CAR[:, j, 2:5], bxb[:, q, nseq:nseq + 3], [("bx", q, s)], [("CAR", j)])
                    else:
                        cp("dve", SO[:, j, 16:32], acx[:, q, 2 + nseq:2 + T], [("acx", q, s)], [("SO", j)])
                        cp("dve", SO[:, j, 64:80], bxb[:, q, 3 + nseq:3 + T], [("bx", q, s)], [("SO", j)])
                        cp("dve", SO[:, j, 96:98], acx[:, q, nseq:nseq + 2], [("acx", q, s)], [("SO", j)])
                        cp("dve", SO[:, j, 98:101], bxb[:, q, nseq:nseq + 3], [("bx", q, s)], [("SO", j)])

            def stage2(j):
                q = j % NSET
                q3 = j % 2
                allk = lambda nm, qq=q: [(nm, qq, 0), (nm, qq, 1)]
                K1, K2, K3 = allk("B1"), allk("B2"), allk("B3")
                b1a, b2a, b3a = B1[:, q, 0:T], B2[:, q, 0:T], B3[:, q, 0:T]
                for sm in sms:
                    s = sm.s
                    c0, c1, n = sm.c0, sm.c1, sm.n
                    rhs = [cbh[:, q, c0:c1]]
                    b_zr = alloc_bank()
                    mm_group(b_zr, n, [wrgs[:, j * 128:(j + 1) * 128]], rhs, [("cbh", q, s), "wrgs"])
                    b_zi = alloc_bank()
                    mm_group(b_zi, n, [wrgs[:, 1024 + j * 128:1024 + (j + 1) * 128]], rhs, [("cbh", q, s), "wrgs"])
                    act(B1[:, q, c0:c1], P[:, b_zr, 0:n], AF.Exp, R_ps(b_zr) + DP_ALL, [("B1", q, s)],
                        scale=-1.0, bias=dcol(0, j))
                    act(B2[:, q, c0:c1], P[:, b_zi, 0:n], AF.Exp, R_ps(b_zi) + DP_ALL, [("B2", q, s)],
                        scale=-1.0, bias=dcol(1, j))
                act(b1a, b1a, AF.Ln, K1 + DP_ALL, K1, bias=1.0)
                act(b1a, b1a, AF.Exp, K1, K1, scale=-1.0)
                act(b3a, b1a, AF.Exp, K1 + DP_ALL, K3, scale=dcol(2, j))
                act(b1a, b1a, AF.Exp, K1 + DP_ALL, K1, scale=dcol(5, j))
                act(b1a, b1a, AF.Ln, K1 + DP_ALL, K1, bias=1.0, scale=-1.0)
                act(b2a, b2a, AF.Ln, K2 + DP_ALL, K2, bias=1.0)
                stt("dve", b1a, b1a, 0.5, b2a, ALU.mult, ALU.subtract, K1 + K2, K1)
                act(b1a, b1a, AF.Exp, K1, K1)
                tt("dve", b2a, b1a, cbb[:, q3, 0:T], ALU.mult, K1 + K2 + [("cb", q3, 0), ("cb", q3, 1)], K2)
                scan("dve", B1[:, q, 0:nseq], B3[:, q, 0:nseq], B2[:, q, 0:nseq], CAR[:, j, 5:6],
                     K3 + K2 + K1 + [("CAR", j)], K1)
                if T > nseq:
                    tt("dve", B1[:, q, nseq:T], B3[:, q, nseq:T], stT[:, j, 80:96], ALU.mult, K3 + ["stT"] + K1, K1)
                    tt("dve", B1[:, q, nseq:T], B1[:, q, nseq:T], B2[:, q, nseq:T], ALU.add, K2 + K1, K1)
                for sm in sms:
                    cp("dve", hid[sm.s][:, 8 + j, 0:sm.n], B1[:, q, sm.c0:sm.c1], K1, [R_yb(j, sm.s)])
                if not last_tile:
                    cp("dve", CAR[:, j, 5:6], B1[:, q, nseq - 1:nseq], K1, [("CAR", j)])
                else:
                    cp("dve", SO[:, j, 80:96], B1[:, q, nseq:T], K1, [("SO", j)])
                    cp("dve", SO[:, j, 101:102], B1[:, q, nseq - 1:nseq], K1, [("SO", j)])

            def wa(j, slot_w, sm):
                stage1(j, sm, slot_w)
                if sm.s == 1:
                    stage2(j)
            run_units(sms, "MA", KC, wa, s1_first=True)
            def wg2(u, slot_w, sm):
                s = sm.s
                wgt = wv(slot_w, 0, KC, 512)
                for bb in range(4):
                    j = 4 * u + bb
                    bk = alloc_bank()
                    mm_group(bk, sm.n, [wgt[:, k, bb * 128:(bb + 1) * 128] for k in range(KC)],
                             [hbf[s][:, k, 0:sm.n] for k in range(KC)],
                             [R_h(k, s) for k in range(KC)] + [("w", slot_w)])
                    gb = j % 2
                    sv = sgb[:, gb, sm.c0:sm.c1]
                    act(sv, P[:, bk, 0:sm.n], AF.Gelu_apprx_tanh, R_ps(bk), [("sgb", gb, s)])
                    ybv = hid[s][:, 8 + j, 0:sm.n]
                    tt("dve", ybv, ybv, sv, ALU.mult, [R_yb(j, s), ("sgb", gb, s)], [R_yb(j, s)])
            run_units(sms, "MG", 2, wg2)
            def wb(oc, slot_w, sm):
                w4 = wring[:, slot_w, 0:4096].rearrange("p (s k c) -> p s k c", s=4, k=KC)
                q = oc % NSET
                s = sm.s
                n = sm.n
                hr = [hbf[s][:, k, 0:n] for k in range(KC)]
                hrd = [R_h(k, s) for k in range(KC)] + [("w", slot_w)]
                b_ga = alloc_bank()
                mm_group(b_ga, n, [w4[:, 2, k, :] for k in range(KC)], hr, hrd)
                b_gb = alloc_bank()
                mm_group(b_gb, n, [w4[:, 3, k, :] for k in range(KC)], hr, hrd)
                sigmoid_act(B1, "B1", q, sm, b_ga, None)
                sigmoid_act(B2, "B2", q, sm, b_gb, None)
                b_ya = alloc_bank()
                mm_group(b_ya, n, [w4[:, 0, k, :] for k in range(KC)],
                         [hid[s][:, k, 0:n] for k in range(KC)],
                         [R_ya(k, s) for k in range(KC)] + [("w", slot_w)])
                b_yb = alloc_bank()
                mm_group(b_yb, n, [w4[:, 1, k, :] for k in range(KC)],
                         [hid[s][:, 8 + k, 0:n] for k in range(KC)],
                         [R_yb(k, s) for k in range(KC)] + [("w", slot_w)])
                k1, k2 = ("B1", q, s), ("B2", q, s)
                b1v, b2v = B1[:, q, sm.c0:sm.c1], B2[:, q, sm.c0:sm.c1]
                tt("dve", b1v, P[:, b_ya, 0:n], b1v, ALU.mult, R_ps(b_ya) + [k1], [k1])
                tt("dve", b2v, P[:, b_yb, 0:n], b2v, ALU.mult, R_ps(b_yb) + [k2], [k2])
                tt("dve", mgbf[s][:, oc, 0:n], b1v, b2v, ALU.add, [k1, k2], [R_mg(oc, s)])
            run_units(sms, "MO", KC, wb)
            bS = {}
            pending = {}

            def wc(u, slot_w, sm):
                s = sm.s
                if s not in bS:
                    bS[s] = alloc_bank()
                    st["held"].add(bS[s])
                    pending[s] = None
                wo = wv(slot_w, 0, KC, 512)
                for bb in range(4):
                    oc = 4 * u + bb
                    bY = alloc_bank()
                    mm_group(bY, sm.n, [wo[:, k, bb * 128:(bb + 1) * 128] for k in range(KC)],
                             [mgbf[s][:, k, 0:sm.n] for k in range(KC)],
                             [R_mg(k, s) for k in range(KC)] + [("w", slot_w)])
                    if pending[s] is not None:
                        pending[s]()
                    pending[s] = evac_with_stats(bY, bS[s], sm, oc, m32, R_m32, pcol(3, oc),
                                                 oc == 0, oc == KC - 1)

            def fc(sm):
                pending[sm.s]()
                post_norm_residual(sm, bS[sm.s], m32, R_m32)
                if next_gvi is not None:
                    norm_to_h(sm, next_gvi)
            run_units(sms, "WO", 2, wc, fc)

        def xs_keys(c0, nr, ns, kcs):
            ss = set()
            if c0 < ns:
                ss.add(0)
            if c0 + nr > ns:
                ss.add(1)
            return [R_xs(k, s) for k in kcs for s in ss]

        def load_tile(row0, T, ns, extra_slots=False):
            nblk = (T + 127) // 128
            for blk in range(nblk):
                c0 = blk * 128
                nr = min(128, T - c0)
                b = load_rows(xin[row0 + c0:row0 + c0 + nr, :], nr, extra_slots)
                if row0 == 0 and blk == 1:
                    st["pace0"] = last_pe_idx()
                if row0 == 0 and blk == 3:
                    st["pace"] = last_pe_idx()
                transpose_in(b, nr,
                             lambda half, c0=c0, nr=nr: cur["xs"][:, 4 * half:4 * half + 4, c0:c0 + nr],
                             lambda half, c0=c0, nr=nr: xs_keys(c0, nr, ns, range(4 * half, 4 * half + 4)))

        def store_tile(row0, T, ns, extra_slots=False):
            nblk = (T + 127) // 128
            ev = []
            for blk in range(nblk):
                c0 = blk * 128
                nr = min(128, T - c0)
                ev.append(store_rows(lambda kc, c0=c0, nr=nr: cur["xs"][:, kc, c0:c0 + nr],
                                     xs_keys(c0, nr, ns, range(KC)), nr,
                                     yout[row0 + c0:row0 + c0 + nr, :], extra_slots))
            return ev

        out_events = []

        def set_buf(ti):
            cur["xb"] = ti % 2
            cur["xs"] = XS[ti % 2]

        def do_load(ti):
            row0, nseq, nsamp = TILES[ti]
            T = nseq + nsamp
            saved = cur["xb"]
            set_buf(ti)
            load_tile(row0, T, NSMAX, extra_slots=(ti == 0))
            io["stg"] = 0
            for sm in [Stm(s, NSMAX, nseq, T) for s in range(2)]:
                norm_stats(sm, rbuf_alt(sm))
            set_buf(saved)

        st["pace"] = None
        do_load(0)
        issue_unit(0, paced=False, dep=st.get("pace0"))
        for g in range(1, NSLOT - 1 - LEAD):
            issue_unit(g)
        for ti, (row0, nseq, nsamp) in enumerate(TILES):
            T = nseq + nsamp
            ns = NSMAX
            set_buf(ti)
            sms = [Stm(s, ns, nseq, T) for s in range(2)]
            for sm in sms:
                norm_apply(sm, 0, rbuf_alt(sm))
                if ti == 0 and sm.s == 0:
                    st["pace"] = len(S.ops) - 1
            ffn(sms, 1, 2)
            last = (ti == len(TILES) - 1)
            if ti == 0:
                S.dma("pool", "c2", lambda e: e.dma_start(out=wrgs[:, :], in_=wrg_d[:, :]), writes=["wrgs"])
                load_states()
            mixer(ti, sms, nseq, T, 4)
            if last:
                out_events.append(store_rows(lambda kc: SO[:, kc, 0:NST_OUT],
                                             [("SO", j) for j in range(KC)] + ["SO_a", "SO_b"], NST_OUT,
                                             stout[0:NST_OUT, :]))
            else:
                do_load(ti + 1)
            ffn(sms, 2, None)
            out_events += store_tile(row0, T, ns, extra_slots=last)
        assert st["unit"] == NUNITS * len(TILES)
        S.wait_all("sp", out_events)

        prog = S.finalize()
        _CACHE['makespan'] = S.makespan

        def replay(engname, e):
            for waits, fn, inc in prog[engname]:
                for k, v in waits:
                    e.wait_ge(sems[k], v)
                if fn is None:
                    continue
                ins = fn(e)
                ins.then_inc(sems[inc[0]], inc[1])

        with nc.Block() as block:
            @block.sync
            def _(e):
                replay("sp", e)

            @block.scalar
            def _(e):
                replay("act", e)

            @block.vector
            def _(e):
                replay("dve", e)

            @block.gpsimd
            def _(e):
                replay("pool", e)

            @block.tensor
            def _(e):
                replay("pe", e)
    return nc


_CACHE = {}


def _get_program():
    if "nc" not in _CACHE:
        _CACHE["nc"] = build_program()
    return _CACHE["nc"]


def kernel(**inputs):
    f = {k: np.asarray(v) for k, v in inputs.items()}
    w = {k: np.ascontiguousarray(f[k][0], dtype=np.float32) for k in (
        "w_ffn1_gate", "w_ffn1_up", "w_ffn1_down", "w_in", "w_out_a", "w_out_b", "w_o",
        "w_ffn2_gate", "w_ffn2_up", "w_ffn2_down")}
    wall = _build_wall(w)
    wrg = np.zeros((128, 2, KC, 128), np.float32)
    for gi, name in enumerate(("w_rg_r", "w_rg_i")):
        wr = f[name][0].astype(np.float32)
        for c in range(KC):
            wrg[0:64, gi, c, 0:64] = wr[2 * c]
            wrg[64:128, gi, c, 64:128] = wr[2 * c + 1]
    wrg = wrg.reshape(128, 2048)
    vecs = [f["g_ffn1_pre"][0], f["g_ffn1_post"][0], f["g_mix_pre"][0], f["g_mix_post"][0],
            f["g_ffn2_pre"][0], f["g_ffn2_post"][0],
            f["conv_a_w"][0, 0], f["conv_a_w"][0, 1], f["conv_a_w"][0, 2],
            f["conv_b_w"][0, 0], f["conv_b_w"][0, 1], f["conv_b_w"][0, 2], f["conv_b_w"][0, 3],
            f["conv_b_b"][0], f["b_rg_r"][0], f["b_rg_i"][0], f["rg_lambda"][0]]
    par = np.stack([np.asarray(v, np.float32).reshape(KC, 128).T for v in vecs], axis=1)
    par = np.ascontiguousarray(par.reshape(128, NPV * 8))
    ident = np.eye(128, dtype=np.float32)
    meta = f["meta_tokens"].astype(np.float32)
    xp = f["x_prompt"].astype(np.float32)
    xsmp = f["x_sample"].astype(np.float32)[:, 0, :]
    sca = f["state_conv_a"][0].astype(np.float32)
    scb = f["state_conv_b"][0].astype(np.float32)
    srg = f["state_rglru"][0].astype(np.float32)
    in_maps = []
    for c in range(NCORES):
        sl = slice(NSAMP * c, NSAMP * (c + 1))
        xin = np.concatenate([meta, xp[c], xsmp[sl]], axis=0)
        stin = np.concatenate([sca[sl, 0], sca[sl, 1], scb[sl, 0], scb[sl, 1], scb[sl, 2], srg[sl]], axis=0)
        in_maps.append({"xin": np.ascontiguousarray(xin), "stin": np.ascontiguousarray(stin),
                        "par": par, "wall": wall, "wrg": wrg, "ident": ident})
    nc = _get_program()
    res = run_bass_kernel_spmd(nc, in_maps, core_ids=list(range(NCORES)))
    B = xp.shape[0]
    y_prompt = np.empty((B, SEQ, D), np.float32)
    y_sample = np.empty((NSAMP * NCORES, 1, D), np.float32)
    nca_p = np.empty((1, B, 2, D), np.float32)
    ncb_p = np.empty((1, B, 3, D), np.float32)
    nh_p = np.empty((1, B, D), np.float32)
    nca_s = np.empty((1, NSAMP * NCORES, 2, D), np.float32)
    ncb_s = np.empty((1, NSAMP * NCORES, 3, D), np.float32)
    nh_s = np.empty((1, NSAMP * NCORES, D), np.float32)
    for c in range(NCORES):
        r = res.results[c]
        yo = r["yout"]
        so = r["stout"]
        sl = slice(NSAMP * c, NSAMP * (c + 1))
        y_prompt[c] = yo[NMETA:NSEQ]
        y_sample[sl, 0] = yo[NSEQ:NTOK]
        nca_s[0, sl, 0] = so[0:16]
        nca_s[0, sl, 1] = so[16:32]
        ncb_s[0, sl, 0] = so[32:48]
        ncb_s[0, sl, 1] = so[48:64]
        ncb_s[0, sl, 2] = so[64:80]
        nh_s[0, sl] = so[80:96]
        nca_p[0, c] = so[96:98]
        ncb_p[0, c] = so[98:101]
        nh_p[0, c] = so[101]
    return (y_prompt, y_sample, nca_p, ncb_p, nh_p, nca_s, ncb_s, nh_s)
```

```python
from contextlib import ExitStack

import numpy as np
import concourse.bass as bass
import concourse.mybir as mybir
from concourse.bass_utils import run_bass_kernel_spmd

F32 = mybir.dt.float32
BF16 = mybir.dt.bfloat16
AF = mybir.ActivationFunctionType
ALU = mybir.AluOpType

NCORES = 8
D = 1024
KC = 8
DFF = 2816
HC = 22
NMETA = 16
SEQ = 2048
NSEQ = NMETA + SEQ
NSAMP = 16
NTOK = NSEQ + NSAMP
TILES = [(0, 688, 0), (688, 688, 0), (1376, 688, 16)]
TMAX = 704
EPS = 1e-6
RG_C = 8.0
NSLOT = 5
SLOT_ELEMS = 4096
NPV = 17
NST_IN = 96
NST_OUT = 102

def _unit_list():
    u = []
    for i in range(11):
        u.append(("GU1", i, 4096))
    for i in range(8):
        u.append(("DN1", i, 2816))
    for i in range(8):
        u.append(("MA", i, 4096))
    for i in range(2):
        u.append(("MG", i, 4096))
    for i in range(8):
        u.append(("MO", i, 4096))
    for i in range(2):
        u.append(("WO", i, 4096))
    for i in range(11):
        u.append(("GU2", i, 4096))
    for i in range(8):
        u.append(("DN2", i, 2816))
    return u


UNIT_LIST = _unit_list()
UNIT_OFF = []
_o = 0
for _k, _i, _n in UNIT_LIST:
    UNIT_OFF.append(_o)
    _o += _n
WTOT = _o
NUNITS = len(UNIT_LIST)


def _fm(w, cols):
    kc = w.shape[0] // 128
    sub = w[:, cols]
    return sub.reshape(kc, 128, sub.shape[1]).transpose(1, 0, 2)


def _build_wall(w):
    wall = np.empty((128, WTOT), np.float32)
    for (kind, i, n), off in zip(UNIT_LIST, UNIT_OFF):
        if kind in ("GU1", "GU2"):
            f = kind[-1]
            g = _fm(w[f"w_ffn{f}_gate"], slice(i * 256, (i + 1) * 256)).reshape(128, -1)
            u = _fm(w[f"w_ffn{f}_up"], slice(i * 256, (i + 1) * 256)).reshape(128, -1)
            blk = np.concatenate([g, u], axis=1)
        elif kind in ("DN1", "DN2"):
            f = kind[-1]
            blk = _fm(w[f"w_ffn{f}_down"], slice(i * 128, (i + 1) * 128)).reshape(128, -1)
        elif kind == "MA":
            parts = [_fm(w["w_in"], slice(s * 1024 + i * 128, s * 1024 + (i + 1) * 128)).reshape(128, -1)
                     for s in range(4)]
            blk = np.concatenate(parts, axis=1)
        elif kind == "MG":
            blk = _fm(w["w_in"], slice(4096 + i * 512, 4096 + (i + 1) * 512)).reshape(128, -1)
        elif kind == "MO":
            parts = [
                _fm(w["w_out_a"], slice(i * 128, (i + 1) * 128)).reshape(128, -1),
                _fm(w["w_out_b"], slice(i * 128, (i + 1) * 128)).reshape(128, -1),
                _fm(w["w_in"], slice(5120 + i * 128, 5120 + (i + 1) * 128)).reshape(128, -1),
                _fm(w["w_in"], slice(6144 + i * 128, 6144 + (i + 1) * 128)).reshape(128, -1),
            ]
            blk = np.concatenate(parts, axis=1)
        elif kind == "WO":
            blk = _fm(w["w_o"], slice(i * 512, (i + 1) * 512)).reshape(128, -1)
        assert blk.shape[1] == n, (kind, blk.shape, n)
        wall[:, off:off + n] = blk
    return wall


SYNC_LAT = 0.20
ACT_TBL_COST = 1.3
WINDOW = 64
READY_EPS = 0.0
PRI_ON = 0
LEAD = 2
PRI_PENALTY = 0.0


class Op:
    __slots__ = ("idx", "eng", "fn", "deps", "dur", "kind", "semkey", "tbl", "issue",
                 "users", "nwait", "ready", "start", "end", "pos", "count", "done", "pri", "qprev", "qusers")

    def __init__(self, idx, eng, fn, deps, dur, kind, semkey=None, tbl=None, issue=0.0):
        self.idx = idx
        self.eng = eng
        self.fn = fn
        self.deps = deps
        self.dur = dur
        self.kind = kind
        self.semkey = semkey
        self.tbl = tbl
        self.issue = issue
        self.users = []
        self.done = False
        self.pri = 0
        self.qprev = None
        self.qusers = []


class Sched:
    ENGS = ("pe", "act", "dve", "pool", "sp")

    def __init__(self):
        self.ops = []
        self.last_write = {}
        self.readers = {}
        self.last_dma_on_queue = {}

    def _deps(self, reads, writes):
        deps = set()
        for r in reads:
            w = self.last_write.get(r)
            if w is not None:
                deps.add(w)
        for w_ in writes:
            w = self.last_write.get(w_)
            if w is not None:
                deps.add(w)
            deps.update(self.readers.get(w_, ()))
        return deps

    def _commit(self, idx, reads, writes):
        for w in writes:
            self.last_write[w] = idx
            self.readers[w] = []
        for r in reads:
            self.readers.setdefault(r, []).append(idx)

    STREAM_KEYS = frozenset(("xs", "hy", "big", "sqb", "rstd", "sgb", "acx", "bx", "cb", "cbh",
                             "A1", "A2", "B1", "B2", "B3"))

    def _pri(self, reads, writes):
        p = 0
        for k in list(writes) + list(reads):
            if isinstance(k, tuple) and k[0] in self.STREAM_KEYS and k[-1] == 1:
                p = 1
        return p

    def op(self, engname, fn, reads=(), writes=(), dur=0.5, tbl=None):
        deps = self._deps(reads, writes)
        o = Op(len(self.ops), engname, fn, deps, dur, "c", tbl=tbl)
        o.pri = self._pri(reads, writes)
        self.ops.append(o)
        self._commit(o.idx, reads, writes)
        return o.idx

    def dma(self, engname, semkey, fn, reads=(), writes=(), dur=3.0, extra_deps=()):
        deps = self._deps(reads, writes)
        deps.update(extra_deps)
        issue = 0.7 if engname == "pool" else 0.1
        o = Op(len(self.ops), engname, fn, deps, dur, "d", semkey=semkey, issue=issue)
        o.qprev = self.last_dma_on_queue.get(engname)
        self.ops.append(o)
        self.last_dma_on_queue[engname] = o.idx
        self._commit(o.idx, reads, writes)
        return o.idx

    def wait_all(self, engname, events):
        o = Op(len(self.ops), engname, None, set(events), 0.0, "w")
        self.ops.append(o)
        return o.idx

    def finalize(self):
        ops = self.ops
        for o in ops:
            o.nwait = len(o.deps)
            o.ready = 0.0
            for d in o.deps:
                ops[d].users.append(o.idx)
            if o.qprev is not None:
                o.nwait += 1
                ops[o.qprev].qusers.append(o.idx)
        queues = {e: [o.idx for o in ops if o.eng == e] for e in self.ENGS}
        heads = {e: 0 for e in self.ENGS}
        t_free = {e: 0.0 for e in self.ENGS}
        cur_tbl = [None]
        order = {e: [] for e in self.ENGS}
        remaining = len(ops)
        while remaining:
            best = None
            for e in self.ENGS:
                q = queues[e]
                h = heads[e]
                while h < len(q) and ops[q[h]].done:
                    h += 1
                heads[e] = h
                seen = 0
                i = h
                tf = t_free[e]
                ebest = None
                while i < len(q) and seen < WINDOW:
                    o = ops[q[i]]
                    i += 1
                    if o.done:
                        continue
                    seen += 1
                    if o.nwait:
                        continue
                    st = max(o.ready, tf)
                    if e == "act" and o.tbl is not None and cur_tbl[0] is not None and o.tbl != cur_tbl[0]:
                        st += ACT_TBL_COST
                    if st <= tf + READY_EPS:
                        key = (tf, PRI_ON * o.pri, o.idx)
                    else:
                        key = (st, 0, o.idx)
                    if ebest is None or key < ebest[0]:
                        ebest = (key, o)
                if ebest is not None and (best is None or ebest[0] < best[0]):
                    best = ebest
            assert best is not None, "scheduler deadlock"
            o = best[1]
            st = max(o.ready, t_free[o.eng])
            if o.eng == "act" and o.tbl is not None and cur_tbl[0] is not None and o.tbl != cur_tbl[0]:
                st += ACT_TBL_COST
            o.start = st
            if o.kind == "d":
                t_free[o.eng] = st + o.issue
                o.end = st + o.issue + o.dur
            else:
                o.end = st + o.dur
                t_free[o.eng] = o.end
                if o.eng == "act" and o.tbl is not None:
                    cur_tbl[0] = o.tbl
            o.done = True
            order[o.eng].append(o)
            remaining -= 1
            for u in o.qusers:
                uo = ops[u]
                uo.nwait -= 1
                if o.start + o.issue > uo.ready:
                    uo.ready = o.start + o.issue
            for u in o.users:
                uo = ops[u]
                uo.nwait -= 1
                lat = SYNC_LAT if (uo.eng != o.eng or o.kind == "d") else 0.05
                if o.end + lat > uo.ready:
                    uo.ready = o.end + lat
        self.makespan = max(o.end for o in ops)
        for e in self.ENGS:
            pos = 0
            for o in order[e]:
                if o.kind == "c":
                    pos += 1
                    o.pos = pos
        dma_count = {}
        for e in self.ENGS:
            for o in order[e]:
                if o.kind == "d":
                    c = dma_count.get(o.semkey, 0) + 16
                    dma_count[o.semkey] = c
                    o.count = c
        prog = {}
        for e in self.ENGS:
            waited = {}
            lst = []
            for o in order[e]:
                need = {}
                for d in o.deps:
                    do = ops[d]
                    if do.kind == "d":
                        k, v = do.semkey, do.count
                    elif do.kind == "c":
                        if do.eng == "pe" and e == "pe":
                            continue
                        k, v = do.eng, do.pos
                    else:
                        continue
                    if need.get(k, 0) < v:
                        need[k] = v
                waits = []
                for k, v in need.items():
                    if waited.get(k, 0) < v:
                        waits.append((k, v))
                        waited[k] = v
                if o.kind == "c":
                    inc = (e, 1)
                elif o.kind == "d":
                    inc = (o.semkey, 16)
                else:
                    inc = None
                lst.append((waits, o.fn, inc))
            prog[e] = lst
        return prog


NSMAX = 352


class Stm:
    def __init__(self, s, ns, nseq, T):
        self.s = s
        self.c0 = 0 if s == 0 else NSMAX
        self.n = NSMAX if s == 0 else T - NSMAX
        self.c1 = self.c0 + self.n
        self.q0 = min(self.c0, nseq)
        self.q1 = min(self.c1, nseq)
        self.samp = self.c1 > nseq
        self.has_end = (self.q1 == nseq)


def build_program():
    nc = bass.Bass("TRN2", target_bir_lowering=False)
    xin = nc.dram_tensor("xin", [NTOK, D], F32, kind="ExternalInput").ap()
    stin = nc.dram_tensor("stin", [NST_IN, D], F32, kind="ExternalInput").ap()
    par_d = nc.dram_tensor("par", [128, NPV * 8], F32, kind="ExternalInput").ap()
    wall = nc.dram_tensor("wall", [128, WTOT], F32, kind="ExternalInput").ap()
    wrg_d = nc.dram_tensor("wrg", [128, 2048], F32, kind="ExternalInput").ap()
    ident_d = nc.dram_tensor("ident", [128, 128], F32, kind="ExternalInput").ap()
    yout = nc.dram_tensor("yout", [NTOK, D], F32, kind="ExternalOutput").ap()
    stout = nc.dram_tensor("stout", [NST_OUT, D], F32, kind="ExternalOutput").ap()

    S = Sched()
    with ExitStack() as ctx:
        def sb(name, shape, dt):
            return ctx.enter_context(nc.sbuf_tensor(name, shape, dt))

        xsA = sb("xsA", [128, KC, TMAX], F32)
        xsB = sb("xsB", [128, KC, TMAX], F32)
        XS = [xsA, xsB]
        cur = {"xs": xsA, "xb": 0}
        hy = sb("hy", [128, 2, KC * NSMAX], F32)
        big = sb("big", [128, 2, HC * NSMAX], BF16)
        wring = sb("wring", [128, NSLOT, SLOT_ELEMS], BF16)
        wrgs = sb("wrgs", [128, 2048], BF16)
        par = sb("par_sb", [128, NPV * 8], F32)
        dpar = sb("dpar", [128, 6 * 8], F32)
        ident = sb("ident_sb", [128, 128], F32)
        ones = sb("ones", [128, 128], BF16)
        cst = sb("cstcol", [128, 2], F32)
        tblw = sb("tblw", [128, 2], F32)
        NSTG = 2
        stg = sb("stg", [128, NSTG, D], F32)
        xin_st = stg
        yout_st = stg
        sqb = sb("sqb", [128, 2, TMAX], BF16)
        rstd = sb("rstd", [128, TMAX], F32)
        sgb = sb("sgb", [128, 2, TMAX], F32)
        ttmp = sgb
        stT = sb("stT", [128, KC, NST_IN], F32)
        SO = sb("SO", [128, KC, 104], F32)
        CAR = sb("CAR", [128, KC, 8], F32)
        NSET = 2
        assert LEAD <= NSET and NSLOT >= LEAD + 3
        acx = sb("acx", [128, NSET, TMAX + 8], F32)
        bxb = sb("bxb", [128, NSET, TMAX + 8], F32)
        cbb = sb("cbb", [128, 2, TMAX], F32)
        cbh = sb("cbh", [128, 2, TMAX], BF16)
        B1 = sb("B1", [128, NSET, TMAX], F32)
        B2 = sb("B2", [128, NSET, TMAX], F32)
        B3 = sb("B3", [128, NSET, TMAX], F32)
        A2 = sb("A2", [128, 1, TMAX], F32)
        P = ctx.enter_context(nc.psum_tensor("P", [128, 8, 512], F32))

        semnames = ["pe", "act", "dve", "pool", "sp", "c0", "c1", "c2", "stg0", "stg1", "stg2", "stg3", "stg4"] + \
                   [f"w{i}" for i in range(NSLOT)]
        sems = {n: ctx.enter_context(nc.semaphore(n)) for n in semnames}

        HALF = KC * NSMAX // 2
        y32 = [hy[:, s, :].rearrange("p (k t) -> p k t", k=KC) for s in range(2)]
        hbf = [hy[:, s, 0:HALF].bitcast(BF16).rearrange("p (k t) -> p k t", k=KC) for s in range(2)]
        mgbf = [hy[:, s, HALF:2 * HALF].bitcast(BF16).rearrange("p (k t) -> p k t", k=KC) for s in range(2)]
        hid = [big[:, s, :].rearrange("p (k t) -> p k t", k=HC) for s in range(2)]
        m32 = [big[:, s, 0:2 * KC * NSMAX].bitcast(F32).rearrange("p (k t) -> p k t", k=KC) for s in range(2)]

        def R_h(kc, s): return ("hy", kc, s)
        def R_mg(kc, s): return ("hy", 8 + kc, s)
        def R_y32(oc, s): return [("hy", 2 * oc, s), ("hy", 2 * oc + 1, s)]
        def R_hid(hc, s): return ("big", hc, s)
        def R_ya(kc, s): return ("big", kc, s)
        def R_yb(kc, s): return ("big", 8 + kc, s)
        def R_m32(oc, s): return [("big", 2 * oc, s), ("big", 2 * oc + 1, s)]
        def R_xs(kc, s): return ("xs", cur["xb"], kc, s)
        def R_ps(bank): return [("ps", bank)]

        def pcol(vi, kc):
            return par[:, vi * 8 + kc:vi * 8 + kc + 1]

        def dcol(vi, kc):
            return dpar[:, vi * 8 + kc:vi * 8 + kc + 1]

        eps_col = cst[:, 0:1]
        one_col = cst[:, 1:2]

        st = {"ps_next": 0, "held": set(), "unit": 0}

        def alloc_bank():
            for _ in range(16):
                b = st["ps_next"] % 8
                st["ps_next"] += 1
                if b not in st["held"]:
                    return b
            raise RuntimeError("no psum bank")

        def last_pe_idx():
            for o in reversed(S.ops):
                if o.eng == "pe":
                    return o.idx
            return None

        def issue_unit(g, paced=True, dep=None):
            if g >= NUNITS * len(TILES):
                return
            extra = [st["pace"]] if (paced and st.get("pace") is not None) else []
            if dep is not None:
                extra = [dep]
            kind, i, n = UNIT_LIST[g % NUNITS]
            off = UNIT_OFF[g % NUNITS]
            slot = g % NSLOT
            dst = wring[:, slot, 0:n]
            src = wall[:, off:off + n]
            S.dma("pool", f"w{slot}", lambda e, dst=dst, src=src: e.dma_start(out=dst, in_=src),
                  reads=(), writes=[("w", slot)], dur=2.0 + n * 128 * 4 / 300e3, extra_deps=extra)

        def next_unit(kind, idx):
            g = st["unit"]
            k, i, n = UNIT_LIST[g % NUNITS]
            assert (k, i) == (kind, idx), ((k, i), (kind, idx))
            st["unit"] += 1
            issue_unit(g + NSLOT - 1 - LEAD)
            return g % NSLOT

        def wv(slot, off, kc_n, cols):
            return wring[:, slot, off:off + kc_n * cols].rearrange("p (k c) -> p k c", k=kc_n)

        def mm_group(bank, n, lhs_list, rhs_list, reads, start=True, stop=True):
            nk = len(lhs_list)
            out = P[:, bank, 0:n]

            def fn(e):
                ins = None
                for k in range(nk):
                    ins = e.matmul(out, lhs_list[k], rhs_list[k],
                                   start=(start and k == 0), stop=(stop and k == nk - 1))
                return ins
            return S.op("pe", fn, reads=reads, writes=R_ps(bank), dur=nk * (n / 2400.0 + 0.010))

        def fsz(ap):
            n = 1
            for d in ap.shape[1:]:
                n *= d
            return n

        TBL = {AF.Silu: "silu", AF.Gelu_apprx_tanh: "gelu", AF.Exp: "lnexp", AF.Ln: "lnexp"}

        def act(out, in_, func, reads, writes, bias=None, scale=None):
            kw = {}
            if bias is not None:
                kw["bias"] = bias
            if scale is not None:
                kw["scale"] = scale
            return S.op("act", lambda e: e.activation(out=out, in_=in_, func=func, **kw),
                        reads=reads, writes=writes, dur=0.22 + fsz(out) * 0.00083, tbl=TBL.get(func))

        def vcost(eng, out, nsrc):
            n = fsz(out)
            if eng == "pool":
                return 0.15 + n / 420.0
            if nsrc >= 2:
                return 0.12 + n / 960.0
            return 0.10 + n / 1800.0

        def tt(eng, out, in0, in1, op, reads, writes):
            return S.op(eng, lambda e: e.tensor_tensor(out=out, in0=in0, in1=in1, op=op),
                        reads=reads, writes=writes, dur=vcost(eng, out, 2))

        def ts(eng, out, in0, s1, s2, op0, op1, reads, writes):
            if op1 is None:
                return S.op(eng, lambda e: e.tensor_scalar(out=out, in0=in0, scalar1=s1, scalar2=None, op0=op0),
                            reads=reads, writes=writes, dur=vcost(eng, out, 1))
            return S.op(eng, lambda e: e.tensor_scalar(out=out, in0=in0, scalar1=s1, scalar2=s2, op0=op0, op1=op1),
                        reads=reads, writes=writes, dur=vcost(eng, out, 1))

        def stt(eng, out, in0, scalar, in1, op0, op1, reads, writes):
            return S.op(eng, lambda e: e.scalar_tensor_tensor(out=out, in0=in0, scalar=scalar, in1=in1,
                                                              op0=op0, op1=op1),
                        reads=reads, writes=writes, dur=vcost(eng, out, 2))

        def scan(eng, out, d0, d1, init, reads, writes):
            return S.op(eng, lambda e: e.tensor_tensor_scan(out=out, data0=d0, data1=d1, initial=init,
                                                            op0=ALU.mult, op1=ALU.add),
                        reads=reads, writes=writes, dur=vcost(eng, out, 2))

        def cp(eng, out, in_, reads, writes):
            if eng == "act":
                return act(out, in_, AF.Identity, reads, writes)
            return S.op(eng, lambda e: e.tensor_copy(out=out, in_=in_), reads=reads, writes=writes,
                        dur=vcost(eng, out, 2))

        S.dma("sp", "c0", lambda e: e.dma_start(out=par[:, :], in_=par_d[:, :]), writes=["par"])
        S.dma("sp", "c1", lambda e: e.dma_start(out=ident[:, :], in_=ident_d[:, :]), writes=["ident"])
        S.op("dve", lambda e: e.memset(ones[:, :], 1.0 / D), writes=["ones"], dur=0.1)
        S.op("dve", lambda e: e.memset(CAR[:, :, :], 0.0), writes=[("CAR", j) for j in range(KC)], dur=0.1)
        S.op("dve", lambda e: e.memset(cst[:, 0:1], EPS), writes=["cst0"], dur=0.1)
        S.op("dve", lambda e: e.memset(tblw[:, :], 0.0), writes=["tblw"], dur=0.1)
        S.op("dve", lambda e: e.memset(cst[:, 1:2], 1.0), writes=["cst1"], dur=0.1)
        ts("dve", dpar[:, 0:8], par[:, 14 * 8:15 * 8], -1.0, None, ALU.mult, None, ["par"], ["dpar0"])
        ts("dve", dpar[:, 8:16], par[:, 15 * 8:16 * 8], -1.0, None, ALU.mult, None, ["par"], ["dpar1"])
        act(dpar[:, 16:24], par[:, 16 * 8:17 * 8], AF.Exp, ["par"], ["dpar2"], scale=-1.0)
        ts("dve", dpar[:, 16:24], dpar[:, 16:24], 1.0, None, ALU.add, None, ["dpar2"], ["dpar2"])
        act(dpar[:, 16:24], dpar[:, 16:24], AF.Ln, ["dpar2"], ["dpar2"])
        ts("dve", dpar[:, 16:24], dpar[:, 16:24], -RG_C, None, ALU.mult, None, ["dpar2"], ["dpar2"])
        ts("dve", dpar[:, 24:32], par[:, 1 * 8:2 * 8], 0.5, None, ALU.mult, None, ["par"], ["dpar3"])
        ts("dve", dpar[:, 32:40], par[:, 5 * 8:6 * 8], 0.5, None, ALU.mult, None, ["par"], ["dpar4"])
        ts("dve", dpar[:, 40:48], dpar[:, 16:24], 2.0, None, ALU.mult, None, ["dpar2"], ["dpar5"])
        DP_ALL = ["par", "dpar0", "dpar1", "dpar2", "dpar3", "dpar4", "dpar5", "cst0", "cst1"]

        io = {"stg": 0}

        ALT_STG = {2: ("B1", B1), 3: ("B2", B2), 4: ("B3", B3)}

        def stg_view(b):
            if b < NSTG:
                return stg[:, b, :]
            return ALT_STG[b][1][:, :, :].rearrange("p a b -> p (a b)")[:, 0:D]

        def stg_keys(b):
            if b < NSTG:
                return [("stg", b)]
            nm = ALT_STG[b][0]
            return [(nm, q_, s_) for q_ in range(2) for s_ in range(2)]

        def load_rows(src_rows, nr, extra_slots=False):
            nslots = NSTG + (2 if extra_slots else 0)
            b = io["stg"] % nslots
            io["stg"] += 1
            dst = stg_view(b)[0:nr, :]
            S.dma("sp", f"stg{b}", lambda e: e.dma_start(out=dst, in_=src_rows),
                  writes=stg_keys(b), dur=2.0 + nr * 4096 / 300e3)
            return b

        def transpose_in(b, nr, dst_fn, dst_res_fn):
            b0 = alloc_bank()
            b1 = alloc_bank()
            banks = (b0, b1)
            sview = stg_view(b)

            def fn(e):
                ins = None
                for kc in range(KC):
                    out = P[:, banks[kc // 4], (kc % 4) * 128:(kc % 4) * 128 + nr]
                    ins = e.transpose(out, sview[0:nr, kc * 128:(kc + 1) * 128], ident[0:nr, 0:nr])
                return ins
            S.op("pe", fn, reads=stg_keys(b) + ["ident"], writes=R_ps(b0) + R_ps(b1), dur=8 * 0.2)
            for half in range(2):
                src = P[:, banks[half], :].rearrange("p (k c) -> p k c", k=4)[:, :, 0:nr]
                eng = "act" if half == 0 else "dve"
                cp(eng, dst_fn(half), src, reads=R_ps(banks[half]), writes=dst_res_fn(half))

        def store_rows(src_fn, src_res, nr, dst_rows, extra_slots=False):
            b0 = alloc_bank()
            b1 = alloc_bank()
            banks = (b0, b1)
            srcs = [src_fn(kc) for kc in range(KC)]

            def fn(e):
                ins = None
                for kc in range(KC):
                    out = P[0:nr, banks[kc // 4], (kc % 4) * 128:(kc % 4 + 1) * 128]
                    ins = e.transpose(out, srcs[kc], ident[:, :])
                return ins
            S.op("pe", fn, reads=src_res + ["ident"], writes=R_ps(b0) + R_ps(b1), dur=8 * 0.2)
            nslots = NSTG + (3 if extra_slots else 0)
            b = io["stg"] % nslots
            io["stg"] += 1
            sv = stg_view(b)
            for half in range(2):
                eng = "act" if half == 0 else "dve"
                cp(eng, sv[0:nr, half * 512:(half + 1) * 512], P[0:nr, banks[half], :],
                   reads=R_ps(banks[half]), writes=stg_keys(b))
            return S.dma("sp", f"stg{b}", lambda e: e.dma_start(out=dst_rows, in_=sv[0:nr, :]),
                         reads=stg_keys(b), dur=2.0 + nr * 4096 / 300e3)

        def load_states():
            b = load_rows(stin[0:NST_IN, :], NST_IN)
            transpose_in(b, NST_IN,
                         lambda half: stT[:, 4 * half:4 * half + 4, :],
                         lambda half: ["stT"])
            cp("dve", SO[:, :, 0:16], stT[:, :, 16:32], ["stT"], ["SO_a"])
            cp("dve", SO[:, :, 32:64], stT[:, :, 48:80], ["stT"], ["SO_b"])

        def rbuf_std(sm):
            return rstd[:, sm.c0:sm.c1], ("rstd", sm.s)

        def rbuf_alt(sm):
            return B3[:, 0, sm.c0:sm.c1], ("B3", 0, sm.s)

        def rms_finish(bank, sm, rb=None):
            rv, rk = rb if rb is not None else rbuf_std(sm)
            act(rv, P[:, bank, 0:sm.n], AF.Ln, R_ps(bank) + DP_ALL, [rk], bias=EPS)
            act(rv, rv, AF.Exp, [rk], [rk], scale=-0.5)

        def norm_to_h(sm, gvi):
            norm_stats(sm, None)
            norm_apply(sm, gvi, None)

        def norm_stats(sm, rb):
            s = sm.s
            bank = alloc_bank()
            st["held"].add(bank)
            for kc in range(KC):
                qb = kc % 2
                sq = sqb[:, qb, sm.c0:sm.c1]
                act(sq, cur["xs"][:, kc, sm.c0:sm.c1], AF.Square, [R_xs(kc, s)], [("sqb", qb, s)])
                mm_group(bank, sm.n, [ones[:, :]], [sq], reads=[("sqb", qb, s), "ones"],
                         start=(kc == 0), stop=(kc == KC - 1))
            rms_finish(bank, sm, rb)
            st["held"].discard(bank)

        def norm_apply(sm, gvi, rb):
            s = sm.s
            rv, rk = rb if rb is not None else rbuf_std(sm)
            for kc in range(KC):
                stt("dve", hbf[s][:, kc, 0:sm.n], cur["xs"][:, kc, sm.c0:sm.c1], pcol(gvi, kc), rv,
                    ALU.mult, ALU.mult, [R_xs(kc, s), rk] + DP_ALL, [R_h(kc, s)])

        def evac_with_stats(bY, bS, sm, oc, dst32, dst_res, gain_col, first, last):
            s = sm.s
            qb = oc % 2
            sq = sqb[:, qb, sm.c0:sm.c1]
            act(sq, P[:, bY, 0:sm.n], AF.Square, R_ps(bY), [("sqb", qb, s)])
            act(dst32[s][:, oc, 0:sm.n], P[:, bY, 0:sm.n], AF.Identity, R_ps(bY) + DP_ALL, dst_res(oc, s),
                scale=gain_col)

            def stats():
                mm_group(bS, sm.n, [ones[:, :]], [sq], reads=[("sqb", qb, s), "ones"], start=first, stop=last)
            return stats

        def post_norm_residual(sm, bS, src32, src_res):
            s = sm.s
            rms_finish(bS, sm)
            st["held"].discard(bS)
            for oc in range(KC):
                tb = oc % 2
                tv = ttmp[:, tb, sm.c0:sm.c1]
                tt("dve", tv, src32[s][:, oc, 0:sm.n], rstd[:, sm.c0:sm.c1], ALU.mult,
                   src_res(oc, s) + [("rstd", s)], [("sgb", tb, s)])
                tt("dve", cur["xs"][:, oc, sm.c0:sm.c1], cur["xs"][:, oc, sm.c0:sm.c1], tv, ALU.add,
                   [R_xs(oc, s), ("sgb", tb, s)], [R_xs(oc, s)])

        def run_units(sms, kind, U, work, finish=None, s1_first=False):
            slots = {}

            def do_s1(it):
                u1 = it - LEAD
                if 0 <= u1 < U:
                    work(u1, slots[u1], sms[1])
                    if u1 == U - 1 and finish is not None:
                        finish(sms[1])

            def do_s0(it):
                if it < U:
                    slots[it] = next_unit(kind, it)
                    work(it, slots[it], sms[0])
                    st["pace"] = last_pe_idx()
                    if it == U - 1 and finish is not None:
                        finish(sms[0])

            for it in range(U + LEAD):
                if s1_first:
                    do_s1(it)
                    do_s0(it)
                else:
                    do_s0(it)
                    do_s1(it)

        def ffn(sms, which, next_gvi):
            ghalf = 3 if which == 1 else 4
            def p1(u, slot_w, sm):
                s = sm.s
                wg = wv(slot_w, 0, KC, 256)
                wu = wv(slot_w, 2048, KC, 256)
                rhs = [hbf[s][:, k, 0:sm.n] for k in range(KC)]
                rd = [R_h(k, s) for k in range(KC)] + [("w", slot_w)]
                for bb in range(2):
                    hc = 2 * u + bb
                    bG = alloc_bank()
                    mm_group(bG, sm.n, [wg[:, k, bb * 128:(bb + 1) * 128] for k in range(KC)], rhs, rd)
                    bU = alloc_bank()
                    mm_group(bU, sm.n, [wu[:, k, bb * 128:(bb + 1) * 128] for k in range(KC)], rhs, rd)
                    gb = hc % 2
                    sv = sgb[:, gb, sm.c0:sm.c1]
                    act(sv, P[:, bG, 0:sm.n], AF.Silu, R_ps(bG), [("sgb", gb, s)])
                    tt("dve", hid[s][:, hc, 0:sm.n], P[:, bU, 0:sm.n], sv, ALU.mult,
                       R_ps(bU) + [("sgb", gb, s)], [R_hid(hc, s)])
            run_units(sms, f"GU{which}", 11, p1)
            act(tblw[:, 0:1], tblw[:, 1:2], AF.Exp,
                ["tblw"] + [("sgb", gb_, s_) for gb_ in range(2) for s_ in range(2)], ["tblw"])
            bS = {}
            pending = {}

            def p2(oc, slot_w, sm):
                s = sm.s
                if s not in bS:
                    bS[s] = alloc_bank()
                    st["held"].add(bS[s])
                    pending[s] = None
                wd = wv(slot_w, 0, HC, 128)
                bY = alloc_bank()
                mm_group(bY, sm.n, [wd[:, k, :] for k in range(HC)],
                         [hid[s][:, k, 0:sm.n] for k in range(HC)],
                         [R_hid(k, s) for k in range(HC)] + [("w", slot_w)])
                if pending[s] is not None:
                    pending[s]()
                pending[s] = evac_with_stats(bY, bS[s], sm, oc, y32, R_y32, dcol(ghalf, oc),
                                             oc == 0, oc == KC - 1)

            def f2(sm):
                pending[sm.s]()
                post_norm_residual(sm, bS[sm.s], y32, R_y32)
                if next_gvi is not None:
                    norm_to_h(sm, next_gvi)
            run_units(sms, f"DN{which}", KC, p2, f2)

        def mixer(ti, sms, nseq, T, next_gvi):
            last_tile = (ti == len(TILES) - 1)

            def sigmoid_act(Bx, nm, q, sm, bank, bias_col):
                bv = Bx[:, q, sm.c0:sm.c1]
                key = (nm, q, sm.s)
                if bias_col is None:
                    act(bv, P[:, bank, 0:sm.n], AF.Exp, R_ps(bank), [key], scale=-1.0)
                else:
                    act(bv, P[:, bank, 0:sm.n], AF.Exp, R_ps(bank) + DP_ALL, [key], scale=-1.0, bias=bias_col)
                act(bv, bv, AF.Ln, [key] + DP_ALL, [key], bias=1.0)
                act(bv, bv, AF.Exp, [key], [key], scale=-1.0)

            def stage1(j, sm, slot_w):
                s = sm.s
                q = j % NSET
                q3 = j % 2
                c0, c1, n = sm.c0, sm.c1, sm.n
                q0, q1 = sm.q0, sm.q1
                w4 = wring[:, slot_w, 0:4096].rearrange("p (s k c) -> p s k c", s=4, k=KC)
                rhs = [hbf[s][:, k, 0:n] for k in range(KC)]
                rd = [R_h(k, s) for k in range(KC)] + [("w", slot_w)]

                def grp(sidx):
                    bk = alloc_bank()
                    mm_group(bk, n, [w4[:, sidx, k, :] for k in range(KC)], rhs, rd)
                    return bk
                b_bx = grp(3)
                b_ac = grp(1)
                b_ax = grp(2)
                b_ab = grp(0)
                if s == 0:
                    cp("dve", acx[:, q, 0:2], CAR[:, j, 0:2], [("CAR", j)], [("acx_h", q)])
                    cp("dve", bxb[:, q, 0:3], CAR[:, j, 2:5], [("CAR", j)], [("bx_h", q)])
                    halo_a = [("acx_h", q)]
                    halo_b = [("bx_h", q)]
                else:
                    halo_a = [("acx", q, 0)]
                    halo_b = [("bx", q, 0)]
                act(bxb[:, q, 3 + c0:3 + c1], P[:, b_bx, 0:n], AF.Identity, R_ps(b_bx), [("bx", q, s)])
                act(acx[:, q, 2 + c0:2 + c1], P[:, b_ac, 0:n], AF.Identity, R_ps(b_ac), [("acx", q, s)])
                cbk = ("cb", q3, s)
                ts("dve", cbb[:, q3, q0:q1], bxb[:, q, q0:q1], pcol(9, j), pcol(13, j), ALU.mult, ALU.add,
                   [("bx", q, s)] + halo_b + DP_ALL, [cbk])
                tt("dve", acx[:, q, 2 + c0:2 + c1], P[:, b_ax, 0:n], acx[:, q, 2 + c0:2 + c1], ALU.mult,
                   R_ps(b_ax) + [("acx", q, s)], [("acx", q, s)])
                for k in (1, 2, 3):
                    stt("dve", cbb[:, q3, q0:q1], bxb[:, q, k + q0:k + q1], pcol(9 + k, j), cbb[:, q3, q0:q1],
                        ALU.mult, ALU.add, [("bx", q, s), cbk] + halo_b + DP_ALL, [cbk])
                if sm.samp:
                    ts("dve", cbb[:, q3, nseq:T], stT[:, j, 32:48], pcol(9, j), pcol(13, j), ALU.mult, ALU.add,
                       ["stT", cbk] + DP_ALL, [cbk])
                    stt("dve", cbb[:, q3, nseq:T], stT[:, j, 48:64], pcol(10, j), cbb[:, q3, nseq:T],
                        ALU.mult, ALU.add, ["stT", cbk] + DP_ALL, [cbk])
                    stt("dve", cbb[:, q3, nseq:T], stT[:, j, 64:80], pcol(11, j), cbb[:, q3, nseq:T],
                        ALU.mult, ALU.add, ["stT", cbk] + DP_ALL, [cbk])
                    stt("dve", cbb[:, q3, nseq:T], bxb[:, q, 3 + nseq:3 + T], pcol(12, j), cbb[:, q3, nseq:T],
                        ALU.mult, ALU.add, [("bx", q, s), cbk] + DP_ALL, [cbk])
                act(cbh[:, q, c0:c1], cbb[:, q3, c0:c1], AF.Identity, [cbk], [("cbh", q, s)])
                a2k = ("A2", 0, s)
                ts("dve", A2[:, 0, q0:q1], acx[:, q, q0:q1], pcol(6, j), None, ALU.mult, None,
                   [("acx", q, s)] + halo_a + DP_ALL, [a2k])
                stt("dve", A2[:, 0, q0:q1], acx[:, q, 1 + q0:1 + q1], pcol(7, j), A2[:, 0, q0:q1], ALU.mult, ALU.add,
                    [("acx", q, s), a2k] + halo_a + DP_ALL, [a2k])
                stt("dve", A2[:, 0, q0:q1], acx[:, q, 2 + q0:2 + q1], pcol(8, j), A2[:, 0, q0:q1], ALU.mult, ALU.add,
                    [("acx", q, s), a2k] + DP_ALL, [a2k])
                if sm.samp:
                    ts("dve", A2[:, 0, nseq:T], stT[:, j, 0:16], pcol(6, j), None, ALU.mult, None,
                       ["stT", a2k] + DP_ALL, [a2k])
                    stt("dve", A2[:, 0, nseq:T], stT[:, j, 16:32], pcol(7, j), A2[:, 0, nseq:T], ALU.mult, ALU.add,
                        ["stT", a2k] + DP_ALL, [a2k])
                    stt("dve", A2[:, 0, nseq:T], acx[:, q, 2 + nseq:2 + T], pcol(8, j), A2[:, 0, nseq:T],
                        ALU.mult, ALU.add, [("acx", q, s), a2k] + DP_ALL, [a2k])
                tt("dve", hid[s][:, j, 0:n], P[:, b_ab, 0:n], A2[:, 0, c0:c1], ALU.mult,
                   R_ps(b_ab) + [a2k], [R_ya(j, s)])
                if sm.has_end:
                    if not last_tile:
                        cp("dve", CAR[:, j, 0:2], acx[:, q, nseq:nseq + 2], [("acx", q, s)], [("CAR", j)])
                        cp("dve", CAR[:, j, 2:5], bxb[:, q, nseq:nseq + 3], [("bx", q, s)], [("CAR", j)])
                    else:
                        cp("dve", SO[:, j, 16:32], acx[:, q, 2 + nseq:2 + T], [("acx", q, s)], [("SO", j)])
                        cp("dve", SO[:, j, 64:80], bxb[:, q, 3 + nseq:3 + T], [("bx", q, s)], [("SO", j)])
                        cp("dve", SO[:, j, 96:98], acx[:, q, nseq:nseq + 2], [("acx", q, s)], [("SO", j)])
                        cp("dve", SO[:, j, 98:101], bxb[:, q, nseq:nseq + 3], [("bx", q, s)], [("SO", j)])

            def stage2(j):
                q = j % NSET
                q3 = j % 2
                allk = lambda nm, qq=q: [(nm, qq, 0), (nm, qq, 1)]
                K1, K2, K3 = allk("B1"), allk("B2"), allk("B3")
                b1a, b2a, b3a = B1[:, q, 0:T], B2[:, q, 0:T], B3[:, q, 0:T]
                for sm in sms:
                    s = sm.s
                    c0, c1, n = sm.c0, sm.c1, sm.n
                    rhs = [cbh[:, q, c0:c1]]
                    b_zr = alloc_bank()
                    mm_group(b_zr, n, [wrgs[:, j * 128:(j + 1) * 128]], rhs, [("cbh", q, s), "wrgs"])
                    b_zi = alloc_bank()
                    mm_group(b_zi, n, [wrgs[:, 1024 + j * 128:1024 + (j + 1) * 128]], rhs, [("cbh", q, s), "wrgs"])
                    act(B1[:, q, c0:c1], P[:, b_zr, 0:n], AF.Exp, R_ps(b_zr) + DP_ALL, [("B1", q, s)],
                        scale=-1.0, bias=dcol(0, j))
                    act(B2[:, q, c0:c1], P[:, b_zi, 0:n], AF.Exp, R_ps(b_zi) + DP_ALL, [("B2", q, s)],
                        scale=-1.0, bias=dcol(1, j))
                act(b1a, b1a, AF.Ln, K1 + DP_ALL, K1, bias=1.0)
                act(b1a, b1a, AF.Exp, K1, K1, scale=-1.0)
                act(b3a, b1a, AF.Exp, K1 + DP_ALL, K3, scale=dcol(2, j))
                act(b1a, b1a, AF.Exp, K1 + DP_ALL, K1, scale=dcol(5, j))
                act(b1a, b1a, AF.Ln, K1 + DP_ALL, K1, bias=1.0, scale=-1.0)
                act(b2a, b2a, AF.Ln, K2 + DP_ALL, K2, bias=1.0)
                stt("dve", b1a, b1a, 0.5, b2a, ALU.mult, ALU.subtract, K1 + K2, K1)
                act(b1a, b1a, AF.Exp, K1, K1)
                tt("dve", b2a, b1a, cbb[:, q3, 0:T], ALU.mult, K1 + K2 + [("cb", q3, 0), ("cb", q3, 1)], K2)
                scan("dve", B1[:, q, 0:nseq], B3[:, q, 0:nseq], B2[:, q, 0:nseq], CAR[:, j, 5:6],
                     K3 + K2 + K1 + [("CAR", j)], K1)
                if T > nseq:
                    tt("dve", B1[:, q, nseq:T], B3[:, q, nseq:T], stT[:, j, 80:96], ALU.mult, K3 + ["stT"] + K1, K1)
                    tt("dve", B1[:, q, nseq:T], B1[:, q, nseq:T], B2[:, q, nseq:T], ALU.add, K2 + K1, K1)
                for sm in sms:
                    cp("dve", hid[sm.s][:, 8 + j, 0:sm.n], B1[:, q, sm.c0:sm.c1], K1, [R_yb(j, sm.s)])
                if not last_tile:
                    cp("dve", CAR[:, j, 5:6], B1[:, q, nseq - 1:nseq], K1, [("CAR", j)])
                else:
                    cp("dve", SO[:, j, 80:96], B1[:, q, nseq:T], K1, [("SO", j)])
                    cp("dve", SO[:, j, 101:102], B1[:, q, nseq - 1:nseq], K1, [("SO", j)])

            def wa(j, slot_w, sm):
                stage1(j, sm, slot_w)
                if sm.s == 1:
                    stage2(j)
            run_units(sms, "MA", KC, wa, s1_first=True)
            def wg2(u, slot_w, sm):
                s = sm.s
                wgt = wv(slot_w, 0, KC, 512)
                for bb in range(4):
                    j = 4 * u + bb
                    bk = alloc_bank()
                    mm_group(bk, sm.n, [wgt[:, k, bb * 128:(bb + 1) * 128] for k in range(KC)],
                             [hbf[s][:, k, 0:sm.n] for k in range(KC)],
                             [R_h(k, s) for k in range(KC)] + [("w", slot_w)])
                    gb = j % 2
                    sv = sgb[:, gb, sm.c0:sm.c1]
                    act(sv, P[:, bk, 0:sm.n], AF.Gelu_apprx_tanh, R_ps(bk), [("sgb", gb, s)])
                    ybv = hid[s][:, 8 + j, 0:sm.n]
                    tt("dve", ybv, ybv, sv, ALU.mult, [R_yb(j, s), ("sgb", gb, s)], [R_yb(j, s)])
            run_units(sms, "MG", 2, wg2)
            def wb(oc, slot_w, sm):
                w4 = wring[:, slot_w, 0:4096].rearrange("p (s k c) -> p s k c", s=4, k=KC)
                q = oc % NSET
                s = sm.s
                n = sm.n
                hr = [hbf[s][:, k, 0:n] for k in range(KC)]
                hrd = [R_h(k, s) for k in range(KC)] + [("w", slot_w)]
                b_ga = alloc_bank()
                mm_group(b_ga, n, [w4[:, 2, k, :] for k in range(KC)], hr, hrd)
                b_gb = alloc_bank()
                mm_group(b_gb, n, [w4[:, 3, k, :] for k in range(KC)], hr, hrd)
                sigmoid_act(B1, "B1", q, sm, b_ga, None)
                sigmoid_act(B2, "B2", q, sm, b_gb, None)
                b_ya = alloc_bank()
                mm_group(b_ya, n, [w4[:, 0, k, :] for k in range(KC)],
                         [hid[s][:, k, 0:n] for k in range(KC)],
                         [R_ya(k, s) for k in range(KC)] + [("w", slot_w)])
                b_yb = alloc_bank()
                mm_group(b_yb, n, [w4[:, 1, k, :] for k in range(KC)],
                         [hid[s][:, 8 + k, 0:n] for k in range(KC)],
                         [R_yb(k, s) for k in range(KC)] + [("w", slot_w)])
                k1, k2 = ("B1", q, s), ("B2", q, s)
                b1v, b2v = B1[:, q, sm.c0:sm.c1], B2[:, q, sm.c0:sm.c1]
                tt("dve", b1v, P[:, b_ya, 0:n], b1v, ALU.mult, R_ps(b_ya) + [k1], [k1])
                tt("dve", b2v, P[:, b_yb, 0:n], b2v, ALU.mult, R_ps(b_yb) + [k2], [k2])
                tt("dve", mgbf[s][:, oc, 0:n], b1v, b2v, ALU.add, [k1, k2], [R_mg(oc, s)])
            run_units(sms, "MO", KC, wb, s1_first=True)
            bS = {}
            pending = {}

            def wc(u, slot_w, sm):
                s = sm.s
                if s not in bS:
                    bS[s] = alloc_bank()
                    st["held"].add(bS[s])
                    pending[s] = None
                wo = wv(slot_w, 0, KC, 512)
                for bb in range(4):
                    oc = 4 * u + bb
                    bY = alloc_bank()
                    mm_group(bY, sm.n, [wo[:, k, bb * 128:(bb + 1) * 128] for k in range(KC)],
                             [mgbf[s][:, k, 0:sm.n] for k in range(KC)],
                             [R_mg(k, s) for k in range(KC)] + [("w", slot_w)])
                    if pending[s] is not None:
                        pending[s]()
                    pending[s] = evac_with_stats(bY, bS[s], sm, oc, m32, R_m32, pcol(3, oc),
                                                 oc == 0, oc == KC - 1)

            def fc(sm):
                pending[sm.s]()
                post_norm_residual(sm, bS[sm.s], m32, R_m32)
                if next_gvi is not None:
                    norm_to_h(sm, next_gvi)
            run_units(sms, "WO", 2, wc, fc)

        def xs_keys(c0, nr, ns, kcs):
            ss = set()
            if c0 < ns:
                ss.add(0)
            if c0 + nr > ns:
                ss.add(1)
            return [R_xs(k, s) for k in kcs for s in ss]

        def load_tile(row0, T, ns, extra_slots=False):
            nblk = (T + 127) // 128
            for blk in range(nblk):
                c0 = blk * 128
                nr = min(128, T - c0)
                b = load_rows(xin[row0 + c0:row0 + c0 + nr, :], nr, extra_slots)
                if row0 == 0 and blk == 1:
                    st["pace0"] = last_pe_idx()
                if row0 == 0 and blk == 3:
                    st["pace"] = last_pe_idx()
                transpose_in(b, nr,
                             lambda half, c0=c0, nr=nr: cur["xs"][:, 4 * half:4 * half + 4, c0:c0 + nr],
                             lambda half, c0=c0, nr=nr: xs_keys(c0, nr, ns, range(4 * half, 4 * half + 4)))

        def store_tile(row0, T, ns, extra_slots=False):
            nblk = (T + 127) // 128
            ev = []
            for blk in range(nblk):
                c0 = blk * 128
                nr = min(128, T - c0)
                ev.append(store_rows(lambda kc, c0=c0, nr=nr: cur["xs"][:, kc, c0:c0 + nr],
                                     xs_keys(c0, nr, ns, range(KC)), nr,
                                     yout[row0 + c0:row0 + c0 + nr, :], extra_slots))
            return ev

        out_events = []

        def set_buf(ti):
            cur["xb"] = ti % 2
            cur["xs"] = XS[ti % 2]

        def do_load(ti):
            row0, nseq, nsamp = TILES[ti]
            T = nseq + nsamp
            saved = cur["xb"]
            set_buf(ti)
            load_tile(row0, T, NSMAX, extra_slots=(ti == 0))
            io["stg"] = 0
            for sm in [Stm(s, NSMAX, nseq, T) for s in range(2)]:
                norm_stats(sm, rbuf_alt(sm))
            set_buf(saved)

        st["pace"] = None
        do_load(0)
        issue_unit(0, paced=False, dep=st.get("pace0"))
        for g in range(1, NSLOT - 1 - LEAD):
            issue_unit(g)
        for ti, (row0, nseq, nsamp) in enumerate(TILES):
            T = nseq + nsamp
            ns = NSMAX
            set_buf(ti)
            sms = [Stm(s, ns, nseq, T) for s in range(2)]
            for sm in sms:
                norm_apply(sm, 0, rbuf_alt(sm))
                if ti == 0 and sm.s == 0:
                    st["pace"] = len(S.ops) - 1
            ffn(sms, 1, 2)
            last = (ti == len(TILES) - 1)
            if ti == 0:
                S.dma("pool", "c2", lambda e: e.dma_start(out=wrgs[:, :], in_=wrg_d[:, :]), writes=["wrgs"])
                load_states()
            mixer(ti, sms, nseq, T, 4)
            if last:
                out_events.append(store_rows(lambda kc: SO[:, kc, 0:NST_OUT],
                                             [("SO", j) for j in range(KC)] + ["SO_a", "SO_b"], NST_OUT,
                                             stout[0:NST_OUT, :]))
            else:
                do_load(ti + 1)
            ffn(sms, 2, None)
            out_events += store_tile(row0, T, ns, extra_slots=last)
        assert st["unit"] == NUNITS * len(TILES)
        S.wait_all("sp", out_events)

        prog = S.finalize()
        _CACHE['makespan'] = S.makespan

        def replay(engname, e):
            for waits, fn, inc in prog[engname]:
                for k, v in waits:
                    e.wait_ge(sems[k], v)
                if fn is None:
                    continue
                ins = fn(e)
                ins.then_inc(sems[inc[0]], inc[1])

        with nc.Block() as block:
            @block.sync
            def _(e):
                replay("sp", e)

            @block.scalar
            def _(e):
                replay("act", e)

            @block.vector
            def _(e):
                replay("dve", e)

            @block.gpsimd
            def _(e):
                replay("pool", e)

            @block.tensor
            def _(e):
                replay("pe", e)
    return nc


_CACHE = {}


def _get_program():
    if "nc" not in _CACHE:
        _CACHE["nc"] = build_program()
    return _CACHE["nc"]


def kernel(**inputs):
    f = {k: np.asarray(v) for k, v in inputs.items()}
    w = {k: np.ascontiguousarray(f[k][0], dtype=np.float32) for k in (
        "w_ffn1_gate", "w_ffn1_up", "w_ffn1_down", "w_in", "w_out_a", "w_out_b", "w_o",
        "w_ffn2_gate", "w_ffn2_up", "w_ffn2_down")}
    wall = _build_wall(w)
    wrg = np.zeros((128, 2, KC, 128), np.float32)
    for gi, name in enumerate(("w_rg_r", "w_rg_i")):
        wr = f[name][0].astype(np.float32)
        for c in range(KC):
            wrg[0:64, gi, c, 0:64] = wr[2 * c]
            wrg[64:128, gi, c, 64:128] = wr[2 * c + 1]
    wrg = wrg.reshape(128, 2048)
    vecs = [f["g_ffn1_pre"][0], f["g_ffn1_post"][0], f["g_mix_pre"][0], f["g_mix_post"][0],
            f["g_ffn2_pre"][0], f["g_ffn2_post"][0],
            f["conv_a_w"][0, 0], f["conv_a_w"][0, 1], f["conv_a_w"][0, 2],
            f["conv_b_w"][0, 0], f["conv_b_w"][0, 1], f["conv_b_w"][0, 2], f["conv_b_w"][0, 3],
            f["conv_b_b"][0], f["b_rg_r"][0], f["b_rg_i"][0], f["rg_lambda"][0]]
    par = np.stack([np.asarray(v, np.float32).reshape(KC, 128).T for v in vecs], axis=1)
    par = np.ascontiguousarray(par.reshape(128, NPV * 8))
    ident = np.eye(128, dtype=np.float32)
    meta = f["meta_tokens"].astype(np.float32)
    xp = f["x_prompt"].astype(np.float32)
    xsmp = f["x_sample"].astype(np.float32)[:, 0, :]
    sca = f["state_conv_a"][0].astype(np.float32)
    scb = f["state_conv_b"][0].astype(np.float32)
    srg = f["state_rglru"][0].astype(np.float32)
    in_maps = []
    for c in range(NCORES):
        sl = slice(NSAMP * c, NSAMP * (c + 1))
        xin = np.concatenate([meta, xp[c], xsmp[sl]], axis=0)
        stin = np.concatenate([sca[sl, 0], sca[sl, 1], scb[sl, 0], scb[sl, 1], scb[sl, 2], srg[sl]], axis=0)
        in_maps.append({"xin": np.ascontiguousarray(xin), "stin": np.ascontiguousarray(stin),
                        "par": par, "wall": wall, "wrg": wrg, "ident": ident})
    nc = _get_program()
    res = run_bass_kernel_spmd(nc, in_maps, core_ids=list(range(NCORES)))
    B = xp.shape[0]
    y_prompt = np.empty((B, SEQ, D), np.float32)
    y_sample = np.empty((NSAMP * NCORES, 1, D), np.float32)
    nca_p = np.empty((1, B, 2, D), np.float32)
    ncb_p = np.empty((1, B, 3, D), np.float32)
    nh_p = np.empty((1, B, D), np.float32)
    nca_s = np.empty((1, NSAMP * NCORES, 2, D), np.float32)
    ncb_s = np.empty((1, NSAMP * NCORES, 3, D), np.float32)
    nh_s = np.empty((1, NSAMP * NCORES, D), np.float32)
    for c in range(NCORES):
        r = res.results[c]
        yo = r["yout"]
        so = r["stout"]
        sl = slice(NSAMP * c, NSAMP * (c + 1))
        y_prompt[c] = yo[NMETA:NSEQ]
        y_sample[sl, 0] = yo[NSEQ:NTOK]
        nca_s[0, sl, 0] = so[0:16]
        nca_s[0, sl, 1] = so[16:32]
        ncb_s[0, sl, 0] = so[32:48]
        ncb_s[0, sl, 1] = so[48:64]
        ncb_s[0, sl, 2] = so[64:80]
        nh_s[0, sl] = so[80:96]
        nca_p[0, c] = so[96:98]
        ncb_p[0, c] = so[98:101]
        nh_p[0, c] = so[101]
    return (y_prompt, y_sample, nca_p, ncb_p, nh_p, nca_s, ncb_s, nh_s)
```

```python
from contextlib import ExitStack

import numpy as np
import concourse.bass as bass
import concourse.mybir as mybir
from concourse.bass_utils import run_bass_kernel_spmd

F32 = mybir.dt.float32
BF16 = mybir.dt.bfloat16
AF = mybir.ActivationFunctionType
ALU = mybir.AluOpType

NCORES = 8
D = 1024
KC = 8
DFF = 2816
HC = 22
NMETA = 16
SEQ = 2048
NSEQ = NMETA + SEQ
NSAMP = 16
NTOK = NSEQ + NSAMP
TILES = [(0, 688, 0), (688, 688, 0), (1376, 688, 16)]
TMAX = 704
EPS = 1e-6
RG_C = 8.0
NSLOT = 5
SLOT_ELEMS = 4096
NPV = 17
NST_IN = 96
NST_OUT = 102

def _unit_list():
    u = []
    for i in range(11):
        u.append(("GU1", i, 4096))
    for i in range(8):
        u.append(("DN1", i, 2816))
    for i in range(8):
        u.append(("MA", i, 4096))
    for i in range(2):
        u.append(("MG", i, 4096))
    for i in range(8):
        u.append(("MO", i, 4096))
    for i in range(2):
        u.append(("WO", i, 4096))
    for i in range(11):
        u.append(("GU2", i, 4096))
    for i in range(8):
        u.append(("DN2", i, 2816))
    return u


UNIT_LIST = _unit_list()
UNIT_OFF = []
_o = 0
for _k, _i, _n in UNIT_LIST:
    UNIT_OFF.append(_o)
    _o += _n
WTOT = _o
NUNITS = len(UNIT_LIST)


def _fm(w, cols):
    kc = w.shape[0] // 128
    sub = w[:, cols]
    return sub.reshape(kc, 128, sub.shape[1]).transpose(1, 0, 2)


def _build_wall(w):
    wall = np.empty((128, WTOT), np.float32)
    for (kind, i, n), off in zip(UNIT_LIST, UNIT_OFF):
        if kind in ("GU1", "GU2"):
            f = kind[-1]
            g = _fm(w[f"w_ffn{f}_gate"], slice(i * 256, (i + 1) * 256)).reshape(128, -1)
            u = _fm(w[f"w_ffn{f}_up"], slice(i * 256, (i + 1) * 256)).reshape(128, -1)
            blk = np.concatenate([g, u], axis=1)
        elif kind in ("DN1", "DN2"):
            f = kind[-1]
            blk = _fm(w[f"w_ffn{f}_down"], slice(i * 128, (i + 1) * 128)).reshape(128, -1)
        elif kind == "MA":
            parts = [_fm(w["w_in"], slice(s * 1024 + i * 128, s * 1024 + (i + 1) * 128)).reshape(128, -1)
                     for s in range(4)]
            blk = np.concatenate(parts, axis=1)
        elif kind == "MG":
            blk = _fm(w["w_in"], slice(4096 + i * 512, 4096 + (i + 1) * 512)).reshape(128, -1)
        elif kind == "MO":
            parts = [
                _fm(w["w_out_a"], slice(i * 128, (i + 1) * 128)).reshape(128, -1),
                _fm(w["w_out_b"], slice(i * 128, (i + 1) * 128)).reshape(128, -1),
                _fm(w["w_in"], slice(5120 + i * 128, 5120 + (i + 1) * 128)).reshape(128, -1),
                _fm(w["w_in"], slice(6144 + i * 128, 6144 + (i + 1) * 128)).reshape(128, -1),
            ]
            blk = np.concatenate(parts, axis=1)
        elif kind == "WO":
            blk = _fm(w["w_o"], slice(i * 512, (i + 1) * 512)).reshape(128, -1)
        assert blk.shape[1] == n, (kind, blk.shape, n)
        wall[:, off:off + n] = blk
    return wall


SYNC_LAT = 0.20
ACT_TBL_COST = 1.3
WINDOW = 64
READY_EPS = 0.0
PRI_ON = 0
LEAD = 2
PRI_PENALTY = 0.0


class Op:
    __slots__ = ("idx", "eng", "fn", "deps", "dur", "kind", "semkey", "tbl", "issue",
                 "users", "nwait", "ready", "start", "end", "pos", "count", "done", "pri", "qprev", "qusers")

    def __init__(self, idx, eng, fn, deps, dur, kind, semkey=None, tbl=None, issue=0.0):
        self.idx = idx
        self.eng = eng
        self.fn = fn
        self.deps = deps
        self.dur = dur
        self.kind = kind
        self.semkey = semkey
        self.tbl = tbl
        self.issue = issue
        self.users = []
        self.done = False
        self.pri = 0
        self.qprev = None
        self.qusers = []


class Sched:
    ENGS = ("pe", "act", "dve", "pool", "sp")

    def __init__(self):
        self.ops = []
        self.last_write = {}
        self.readers = {}
        self.last_dma_on_queue = {}

    def _deps(self, reads, writes):
        deps = set()
        for r in reads:
            w = self.last_write.get(r)
            if w is not None:
                deps.add(w)
        for w_ in writes:
            w = self.last_write.get(w_)
            if w is not None:
                deps.add(w)
            deps.update(self.readers.get(w_, ()))
        return deps

    def _commit(self, idx, reads, writes):
        for w in writes:
            self.last_write[w] = idx
            self.readers[w] = []
        for r in reads:
            self.readers.setdefault(r, []).append(idx)

    STREAM_KEYS = frozenset(("xs", "hy", "big", "sqb", "rstd", "sgb", "acx", "bx", "cb", "cbh",
                             "A1", "A2", "B1", "B2", "B3"))

    def _pri(self, reads, writes):
        p = 0
        for k in list(writes) + list(reads):
            if isinstance(k, tuple) and k[0] in self.STREAM_KEYS and k[-1] == 1:
                p = 1
        return p

    def op(self, engname, fn, reads=(), writes=(), dur=0.5, tbl=None):
        deps = self._deps(reads, writes)
        o = Op(len(self.ops), engname, fn, deps, dur, "c", tbl=tbl)
        o.pri = self._pri(reads, writes)
        self.ops.append(o)
        self._commit(o.idx, reads, writes)
        return o.idx

    def dma(self, engname, semkey, fn, reads=(), writes=(), dur=3.0, extra_deps=()):
        deps = self._deps(reads, writes)
        deps.update(extra_deps)
        issue = 0.7 if engname == "pool" else 0.1
        o = Op(len(self.ops), engname, fn, deps, dur, "d", semkey=semkey, issue=issue)
        o.qprev = self.last_dma_on_queue.get(engname)
        self.ops.append(o)
        self.last_dma_on_queue[engname] = o.idx
        self._commit(o.idx, reads, writes)
        return o.idx

    def wait_all(self, engname, events):
        o = Op(len(self.ops), engname, None, set(events), 0.0, "w")
        self.ops.append(o)
        return o.idx

    def finalize(self):
        ops = self.ops
        for o in ops:
            o.nwait = len(o.deps)
            o.ready = 0.0
            for d in o.deps:
                ops[d].users.append(o.idx)
            if o.qprev is not None:
                o.nwait += 1
                ops[o.qprev].qusers.append(o.idx)
        queues = {e: [o.idx for o in ops if o.eng == e] for e in self.ENGS}
        heads = {e: 0 for e in self.ENGS}
        t_free = {e: 0.0 for e in self.ENGS}
        cur_tbl = [None]
        order = {e: [] for e in self.ENGS}
        remaining = len(ops)
        while remaining:
            best = None
            for e in self.ENGS:
                q = queues[e]
                h = heads[e]
                while h < len(q) and ops[q[h]].done:
                    h += 1
                heads[e] = h
                seen = 0
                i = h
                tf = t_free[e]
                ebest = None
                while i < len(q) and seen < WINDOW:
                    o = ops[q[i]]
                    i += 1
                    if o.done:
                        continue
                    seen += 1
                    if o.nwait:
                        continue
                    st = max(o.ready, tf)
                    if e == "act" and o.tbl is not None and cur_tbl[0] is not None and o.tbl != cur_tbl[0]:
                        st += ACT_TBL_COST
                    if st <= tf + READY_EPS:
                        key = (tf, PRI_ON * o.pri, o.idx)
                    else:
                        key = (st, 0, o.idx)
                    if ebest is None or key < ebest[0]:
                        ebest = (key, o)
                if ebest is not None and (best is None or ebest[0] < best[0]):
                    best = ebest
            assert best is not None, "scheduler deadlock"
            o = best[1]
            st = max(o.ready, t_free[o.eng])
            if o.eng == "act" and o.tbl is not None and cur_tbl[0] is not None and o.tbl != cur_tbl[0]:
                st += ACT_TBL_COST
            o.start = st
            if o.kind == "d":
                t_free[o.eng] = st + o.issue
                o.end = st + o.issue + o.dur
            else:
                o.end = st + o.dur
                t_free[o.eng] = o.end
                if o.eng == "act" and o.tbl is not None:
                    cur_tbl[0] = o.tbl
            o.done = True
            order[o.eng].append(o)
            remaining -= 1
            for u in o.qusers:
                uo = ops[u]
                uo.nwait -= 1
                if o.start + o.issue > uo.ready:
                    uo.ready = o.start + o.issue
            for u in o.users:
                uo = ops[u]
                uo.nwait -= 1
                lat = SYNC_LAT if (uo.eng != o.eng or o.kind == "d") else 0.05
                if o.end + lat > uo.ready:
                    uo.ready = o.end + lat
        self.makespan = max(o.end for o in ops)
        for e in self.ENGS:
            pos = 0
            for o in order[e]:
                if o.kind == "c":
                    pos += 1
                    o.pos = pos
        dma_count = {}
        for e in self.ENGS:
            for o in order[e]:
                if o.kind == "d":
                    c = dma_count.get(o.semkey, 0) + 16
                    dma_count[o.semkey] = c
                    o.count = c
        prog = {}
        for e in self.ENGS:
            waited = {}
            lst = []
            for o in order[e]:
                need = {}
                for d in o.deps:
                    do = ops[d]
                    if do.kind == "d":
                        k, v = do.semkey, do.count
                    elif do.kind == "c":
                        if do.eng == "pe" and e == "pe":
                            continue
                        k, v = do.eng, do.pos
                    else:
                        continue
                    if need.get(k, 0) < v:
                        need[k] = v
                waits = []
                for k, v in need.items():
                    if waited.get(k, 0) < v:
                        waits.append((k, v))
                        waited[k] = v
                if o.kind == "c":
                    inc = (e, 1)
                elif o.kind == "d":
                    inc = (o.semkey, 16)
                else:
                    inc = None
                lst.append((waits, o.fn, inc))
            prog[e] = lst
        return prog


NSMAX = 352


class Stm:
    def __init__(self, s, ns, nseq, T):
        self.s = s
        self.c0 = 0 if s == 0 else NSMAX
        self.n = NSMAX if s == 0 else T - NSMAX
        self.c1 = self.c0 + self.n
        self.q0 = min(self.c0, nseq)
        self.q1 = min(self.c1, nseq)
        self.samp = self.c1 > nseq
        self.has_end = (self.q1 == nseq)


def build_program():
    nc = bass.Bass("TRN2", target_bir_lowering=False)
    xin = nc.dram_tensor("xin", [NTOK, D], F32, kind="ExternalInput").ap()
    stin = nc.dram_tensor("stin", [NST_IN, D], F32, kind="ExternalInput").ap()
    par_d = nc.dram_tensor("par", [128, NPV * 8], F32, kind="ExternalInput").ap()
    wall = nc.dram_tensor("wall", [128, WTOT], F32, kind="ExternalInput").ap()
    wrg_d = nc.dram_tensor("wrg", [128, 2048], F32, kind="ExternalInput").ap()
    ident_d = nc.dram_tensor("ident", [128, 128], F32, kind="ExternalInput").ap()
    yout = nc.dram_tensor("yout", [NTOK, D], F32, kind="ExternalOutput").ap()
    stout = nc.dram_tensor("stout", [NST_OUT, D], F32, kind="ExternalOutput").ap()

    S = Sched()
    with ExitStack() as ctx:
        def sb(name, shape, dt):
            return ctx.enter_context(nc.sbuf_tensor(name, shape, dt))

        xsA = sb("xsA", [128, KC, TMAX], F32)
        xsB = sb("xsB", [128, KC, TMAX], F32)
        XS = [xsA, xsB]
        cur = {"xs": xsA, "xb": 0}
        hy = sb("hy", [128, 2, KC * NSMAX], F32)
        big = sb("big", [128, 2, HC * NSMAX], BF16)
        wring = sb("wring", [128, NSLOT, SLOT_ELEMS], BF16)
        wrgs = sb("wrgs", [128, 2048], BF16)
        par = sb("par_sb", [128, NPV * 8], F32)
        dpar = sb("dpar", [128, 6 * 8], F32)
        ident = sb("ident_sb", [128, 128], F32)
        ones = sb("ones", [128, 128], BF16)
        cst = sb("cstcol", [128, 2], F32)
        tblw = sb("tblw", [128, 2], F32)
        NSTG = 2
        stg = sb("stg", [128, NSTG, D], F32)
        xin_st = stg
        yout_st = stg
        sqb = sb("sqb", [128, 2, TMAX], BF16)
        rstd = sb("rstd", [128, TMAX], F32)
        sgb = sb("sgb", [128, 2, TMAX], F32)
        ttmp = sgb
        stT = sb("stT", [128, KC, NST_IN], F32)
        SO = sb("SO", [128, KC, 104], F32)
        CAR = sb("CAR", [128, KC, 8], F32)
        NSET = 2
        assert LEAD <= NSET and NSLOT >= LEAD + 3
        acx = sb("acx", [128, NSET, TMAX + 8], F32)
        bxb = sb("bxb", [128, NSET, TMAX + 8], F32)
        cbb = sb("cbb", [128, 2, TMAX], F32)
        cbh = sb("cbh", [128, 2, TMAX], BF16)
        B1 = sb("B1", [128, NSET, TMAX], F32)
        B2 = sb("B2", [128, NSET, TMAX], F32)
        B3 = sb("B3", [128, NSET, TMAX], F32)
        A2 = sb("A2", [128, 1, TMAX], F32)
        P = ctx.enter_context(nc.psum_tensor("P", [128, 8, 512], F32))

        semnames = ["pe", "act", "dve", "pool", "sp", "c0", "c1", "c2", "stg0", "stg1", "stg2", "stg3", "stg4"] + \
                   [f"w{i}" for i in range(NSLOT)]
        sems = {n: ctx.enter_context(nc.semaphore(n)) for n in semnames}

        HALF = KC * NSMAX // 2
        y32 = [hy[:, s, :].rearrange("p (k t) -> p k t", k=KC) for s in range(2)]
        hbf = [hy[:, s, 0:HALF].bitcast(BF16).rearrange("p (k t) -> p k t", k=KC) for s in range(2)]
        mgbf = [hy[:, s, HALF:2 * HALF].bitcast(BF16).rearrange("p (k t) -> p k t", k=KC) for s in range(2)]
        hid = [big[:, s, :].rearrange("p (k t) -> p k t", k=HC) for s in range(2)]
        m32 = [big[:, s, 0:2 * KC * NSMAX].bitcast(F32).rearrange("p (k t) -> p k t", k=KC) for s in range(2)]

        def R_h(kc, s): return ("hy", kc, s)
        def R_mg(kc, s): return ("hy", 8 + kc, s)
        def R_y32(oc, s): return [("hy", 2 * oc, s), ("hy", 2 * oc + 1, s)]
        def R_hid(hc, s): return ("big", hc, s)
        def R_ya(kc, s): return ("big", kc, s)
        def R_yb(kc, s): return ("big", 8 + kc, s)
        def R_m32(oc, s): return [("big", 2 * oc, s), ("big", 2 * oc + 1, s)]
        def R_xs(kc, s): return ("xs", cur["xb"], kc, s)
        def R_ps(bank): return [("ps", bank)]

        def pcol(vi, kc):
            return par[:, vi * 8 + kc:vi * 8 + kc + 1]

        def dcol(vi, kc):
            return dpar[:, vi * 8 + kc:vi * 8 + kc + 1]

        eps_col = cst[:, 0:1]
        one_col = cst[:, 1:2]

        st = {"ps_next": 0, "held": set(), "unit": 0}

        def alloc_bank():
            for _ in range(16):
                b = st["ps_next"] % 8
                st["ps_next"] += 1
                if b not in st["held"]:
                    return b
            raise RuntimeError("no psum bank")

        def last_pe_idx():
            for o in reversed(S.ops):
                if o.eng == "pe":
                    return o.idx
            return None

        def issue_unit(g, paced=True, dep=None):
            if g >= NUNITS * len(TILES):
                return
            extra = [st["pace"]] if (paced and st.get("pace") is not None) else []
            if dep is not None:
                extra = [dep]
            kind, i, n = UNIT_LIST[g % NUNITS]
            off = UNIT_OFF[g % NUNITS]
            slot = g % NSLOT
            dst = wring[:, slot, 0:n]
            src = wall[:, off:off + n]
            S.dma("pool", f"w{slot}", lambda e, dst=dst, src=src: e.dma_start(out=dst, in_=src),
                  reads=(), writes=[("w", slot)], dur=2.0 + n * 128 * 4 / 300e3, extra_deps=extra)

        def next_unit(kind, idx):
            g = st["unit"]
            k, i, n = UNIT_LIST[g % NUNITS]
            assert (k, i) == (kind, idx), ((k, i), (kind, idx))
            st["unit"] += 1
            issue_unit(g + NSLOT - 1 - LEAD)
            return g % NSLOT

        def wv(slot, off, kc_n, cols):
            return wring[:, slot, off:off + kc_n * cols].rearrange("p (k c) -> p k c", k=kc_n)

        def mm_group(bank, n, lhs_list, rhs_list, reads, start=True, stop=True):
            nk = len(lhs_list)
            out = P[:, bank, 0:n]

            def fn(e):
                ins = None
                for k in range(nk):
                    ins = e.matmul(out, lhs_list[k], rhs_list[k],
                                   start=(start and k == 0), stop=(stop and k == nk - 1))
                return ins
            return S.op("pe", fn, reads=reads, writes=R_ps(bank), dur=nk * (n / 2400.0 + 0.010))

        def fsz(ap):
            n = 1
            for d in ap.shape[1:]:
                n *= d
            return n

        TBL = {AF.Silu: "silu", AF.Gelu_apprx_tanh: "gelu", AF.Exp: "lnexp", AF.Ln: "lnexp"}

        def act(out, in_, func, reads, writes, bias=None, scale=None):
            kw = {}
            if bias is not None:
                kw["bias"] = bias
            if scale is not None:
                kw["scale"] = scale
            nptr = sum(1 for v in (bias, scale) if v is not None and not isinstance(v, (int, float)))
            return S.op("act", lambda e: e.activation(out=out, in_=in_, func=func, **kw),
                        reads=reads, writes=writes, dur=0.12 + 0.093 * nptr + fsz(out) * 0.00083,
                        tbl=TBL.get(func))

        def vcost(eng, out, nsrc):
            n = fsz(out)
            if eng == "pool":
                return 0.15 + n / 420.0
            if nsrc >= 2:
                return 0.12 + n / 960.0
            return 0.10 + n / 1800.0

        def tt(eng, out, in0, in1, op, reads, writes):
            return S.op(eng, lambda e: e.tensor_tensor(out=out, in0=in0, in1=in1, op=op),
                        reads=reads, writes=writes, dur=vcost(eng, out, 2))

        def ts(eng, out, in0, s1, s2, op0, op1, reads, writes):
            if op1 is None:
                return S.op(eng, lambda e: e.tensor_scalar(out=out, in0=in0, scalar1=s1, scalar2=None, op0=op0),
                            reads=reads, writes=writes, dur=vcost(eng, out, 1))
            return S.op(eng, lambda e: e.tensor_scalar(out=out, in0=in0, scalar1=s1, scalar2=s2, op0=op0, op1=op1),
                        reads=reads, writes=writes, dur=vcost(eng, out, 1))

        def stt(eng, out, in0, scalar, in1, op0, op1, reads, writes):
            return S.op(eng, lambda e: e.scalar_tensor_tensor(out=out, in0=in0, scalar=scalar, in1=in1,
                                                              op0=op0, op1=op1),
                        reads=reads, writes=writes, dur=vcost(eng, out, 2))

        def scan(eng, out, d0, d1, init, reads, writes):
            return S.op(eng, lambda e: e.tensor_tensor_scan(out=out, data0=d0, data1=d1, initial=init,
                                                            op0=ALU.mult, op1=ALU.add),
                        reads=reads, writes=writes, dur=vcost(eng, out, 2))

        def cp(eng, out, in_, reads, writes):
            if eng == "act":
                return act(out, in_, AF.Identity, reads, writes)
            return S.op(eng, lambda e: e.tensor_copy(out=out, in_=in_), reads=reads, writes=writes,
                        dur=vcost(eng, out, 2))

        S.dma("sp", "c0", lambda e: e.dma_start(out=par[:, :], in_=par_d[:, :]), writes=["par"])
        S.dma("sp", "c1", lambda e: e.dma_start(out=ident[:, :], in_=ident_d[:, :]), writes=["ident"])
        S.op("dve", lambda e: e.memset(ones[:, :], 1.0 / D), writes=["ones"], dur=0.1)
        S.op("dve", lambda e: e.memset(CAR[:, :, :], 0.0), writes=[("CAR", j) for j in range(KC)], dur=0.1)
        S.op("dve", lambda e: e.memset(cst[:, 0:1], EPS), writes=["cst0"], dur=0.1)
        S.op("dve", lambda e: e.memset(tblw[:, :], 0.0), writes=["tblw"], dur=0.1)
        S.op("dve", lambda e: e.memset(cst[:, 1:2], 1.0), writes=["cst1"], dur=0.1)
        ts("dve", dpar[:, 0:8], par[:, 14 * 8:15 * 8], -1.0, None, ALU.mult, None, ["par"], ["dpar0"])
        ts("dve", dpar[:, 8:16], par[:, 15 * 8:16 * 8], -1.0, None, ALU.mult, None, ["par"], ["dpar1"])
        act(dpar[:, 16:24], par[:, 16 * 8:17 * 8], AF.Exp, ["par"], ["dpar2"], scale=-1.0)
        ts("dve", dpar[:, 16:24], dpar[:, 16:24], 1.0, None, ALU.add, None, ["dpar2"], ["dpar2"])
        act(dpar[:, 16:24], dpar[:, 16:24], AF.Ln, ["dpar2"], ["dpar2"])
        ts("dve", dpar[:, 16:24], dpar[:, 16:24], -RG_C, None, ALU.mult, None, ["dpar2"], ["dpar2"])
        ts("dve", dpar[:, 24:32], par[:, 1 * 8:2 * 8], 0.5, None, ALU.mult, None, ["par"], ["dpar3"])
        ts("dve", dpar[:, 32:40], par[:, 5 * 8:6 * 8], 0.5, None, ALU.mult, None, ["par"], ["dpar4"])
        ts("dve", dpar[:, 40:48], dpar[:, 16:24], 2.0, None, ALU.mult, None, ["dpar2"], ["dpar5"])
        DP_ALL = ["par", "dpar0", "dpar1", "dpar2", "dpar3", "dpar4", "dpar5", "cst0", "cst1"]

        io = {"stg": 0}

        ALT_STG = {2: ("B1", B1), 3: ("B2", B2), 4: ("B3", B3)}

        def stg_view(b):
            if b < NSTG:
                return stg[:, b, :]
            return ALT_STG[b][1][:, :, :].rearrange("p a b -> p (a b)")[:, 0:D]

        def stg_keys(b):
            if b < NSTG:
                return [("stg", b)]
            nm = ALT_STG[b][0]
            return [(nm, q_, s_) for q_ in range(2) for s_ in range(2)]

        def load_rows(src_rows, nr, extra_slots=False):
            nslots = NSTG + (2 if extra_slots else 0)
            b = io["stg"] % nslots
            io["stg"] += 1
            dst = stg_view(b)[0:nr, :]
            S.dma("sp", f"stg{b}", lambda e: e.dma_start(out=dst, in_=src_rows),
                  writes=stg_keys(b), dur=2.0 + nr * 4096 / 300e3)
            return b

        def transpose_in(b, nr, dst_fn, dst_res_fn):
            b0 = alloc_bank()
            b1 = alloc_bank()
            banks = (b0, b1)
            sview = stg_view(b)

            def fn(e):
                ins = None
                for kc in range(KC):
                    out = P[:, banks[kc // 4], (kc % 4) * 128:(kc % 4) * 128 + nr]
                    ins = e.transpose(out, sview[0:nr, kc * 128:(kc + 1) * 128], ident[0:nr, 0:nr])
                return ins
            S.op("pe", fn, reads=stg_keys(b) + ["ident"], writes=R_ps(b0) + R_ps(b1), dur=8 * 0.2)
            for half in range(2):
                src = P[:, banks[half], :].rearrange("p (k c) -> p k c", k=4)[:, :, 0:nr]
                eng = "act" if half == 0 else "dve"
                cp(eng, dst_fn(half), src, reads=R_ps(banks[half]), writes=dst_res_fn(half))

        def store_rows(src_fn, src_res, nr, dst_rows, extra_slots=False):
            b0 = alloc_bank()
            b1 = alloc_bank()
            banks = (b0, b1)
            srcs = [src_fn(kc) for kc in range(KC)]

            def fn(e):
                ins = None
                for kc in range(KC):
                    out = P[0:nr, banks[kc // 4], (kc % 4) * 128:(kc % 4 + 1) * 128]
                    ins = e.transpose(out, srcs[kc], ident[:, :])
                return ins
            S.op("pe", fn, reads=src_res + ["ident"], writes=R_ps(b0) + R_ps(b1), dur=8 * 0.2)
            nslots = NSTG + (3 if extra_slots else 0)
            b = io["stg"] % nslots
            io["stg"] += 1
            sv = stg_view(b)
            for half in range(2):
                eng = "act" if half == 0 else "dve"
                cp(eng, sv[0:nr, half * 512:(half + 1) * 512], P[0:nr, banks[half], :],
                   reads=R_ps(banks[half]), writes=stg_keys(b))
            return S.dma("sp", f"stg{b}", lambda e: e.dma_start(out=dst_rows, in_=sv[0:nr, :]),
                         reads=stg_keys(b), dur=2.0 + nr * 4096 / 300e3)

        def load_states():
            b = load_rows(stin[0:NST_IN, :], NST_IN)
            transpose_in(b, NST_IN,
                         lambda half: stT[:, 4 * half:4 * half + 4, :],
                         lambda half: ["stT"])
            cp("dve", SO[:, :, 0:16], stT[:, :, 16:32], ["stT"], ["SO_a"])
            cp("dve", SO[:, :, 32:64], stT[:, :, 48:80], ["stT"], ["SO_b"])

        def rbuf_std(sm):
            return rstd[:, sm.c0:sm.c1], ("rstd", sm.s)

        def rbuf_alt(sm):
            return B3[:, 0, sm.c0:sm.c1], ("B3", 0, sm.s)

        def rms_finish(bank, sm, rb=None):
            rv, rk = rb if rb is not None else rbuf_std(sm)
            act(rv, P[:, bank, 0:sm.n], AF.Ln, R_ps(bank) + DP_ALL, [rk], bias=EPS)
            act(rv, rv, AF.Exp, [rk], [rk], scale=-0.5)

        def norm_to_h(sm, gvi):
            norm_stats(sm, None)
            norm_apply(sm, gvi, None)

        def norm_stats(sm, rb):
            s = sm.s
            bank = alloc_bank()
            st["held"].add(bank)
            for kc in range(KC):
                qb = kc % 2
                sq = sqb[:, qb, sm.c0:sm.c1]
                act(sq, cur["xs"][:, kc, sm.c0:sm.c1], AF.Square, [R_xs(kc, s)], [("sqb", qb, s)])
                mm_group(bank, sm.n, [ones[:, :]], [sq], reads=[("sqb", qb, s), "ones"],
                         start=(kc == 0), stop=(kc == KC - 1))
            rms_finish(bank, sm, rb)
            st["held"].discard(bank)

        def norm_apply(sm, gvi, rb):
            s = sm.s
            rv, rk = rb if rb is not None else rbuf_std(sm)
            for kc in range(KC):
                stt("dve", hbf[s][:, kc, 0:sm.n], cur["xs"][:, kc, sm.c0:sm.c1], pcol(gvi, kc), rv,
                    ALU.mult, ALU.mult, [R_xs(kc, s), rk] + DP_ALL, [R_h(kc, s)])

        def evac_with_stats(bY, bS, sm, oc, dst32, dst_res, gain_col, first, last):
            s = sm.s
            qb = oc % 2
            sq = sqb[:, qb, sm.c0:sm.c1]
            act(sq, P[:, bY, 0:sm.n], AF.Square, R_ps(bY), [("sqb", qb, s)])
            act(dst32[s][:, oc, 0:sm.n], P[:, bY, 0:sm.n], AF.Identity, R_ps(bY) + DP_ALL, dst_res(oc, s),
                scale=gain_col)

            def stats():
                mm_group(bS, sm.n, [ones[:, :]], [sq], reads=[("sqb", qb, s), "ones"], start=first, stop=last)
            return stats

        def post_norm_residual(sm, bS, src32, src_res):
            s = sm.s
            rms_finish(bS, sm)
            st["held"].discard(bS)
            for oc in range(KC):
                tb = oc % 2
                tv = ttmp[:, tb, sm.c0:sm.c1]
                tt("dve", tv, src32[s][:, oc, 0:sm.n], rstd[:, sm.c0:sm.c1], ALU.mult,
                   src_res(oc, s) + [("rstd", s)], [("sgb", tb, s)])
                tt("dve", cur["xs"][:, oc, sm.c0:sm.c1], cur["xs"][:, oc, sm.c0:sm.c1], tv, ALU.add,
                   [R_xs(oc, s), ("sgb", tb, s)], [R_xs(oc, s)])

        def run_units(sms, kind, U, work, finish=None, s1_first=False):
            slots = {}

            def do_s1(it):
                u1 = it - LEAD
                if 0 <= u1 < U:
                    work(u1, slots[u1], sms[1])
                    if u1 == U - 1 and finish is not None:
                        finish(sms[1])

            def do_s0(it):
                if it < U:
                    slots[it] = next_unit(kind, it)
                    work(it, slots[it], sms[0])
                    st["pace"] = last_pe_idx()
                    if it == U - 1 and finish is not None:
                        finish(sms[0])

            for it in range(U + LEAD):
                if s1_first:
                    do_s1(it)
                    do_s0(it)
                else:
                    do_s0(it)
                    do_s1(it)

        def ffn(sms, which, next_gvi):
            ghalf = 3 if which == 1 else 4
            def p1(u, slot_w, sm):
                s = sm.s
                wg = wv(slot_w, 0, KC, 256)
                wu = wv(slot_w, 2048, KC, 256)
                rhs = [hbf[s][:, k, 0:sm.n] for k in range(KC)]
                rd = [R_h(k, s) for k in range(KC)] + [("w", slot_w)]
                for bb in range(2):
                    hc = 2 * u + bb
                    bG = alloc_bank()
                    mm_group(bG, sm.n, [wg[:, k, bb * 128:(bb + 1) * 128] for k in range(KC)], rhs, rd)
                    bU = alloc_bank()
                    mm_group(bU, sm.n, [wu[:, k, bb * 128:(bb + 1) * 128] for k in range(KC)], rhs, rd)
                    gb = hc % 2
                    sv = sgb[:, gb, sm.c0:sm.c1]
                    act(sv, P[:, bG, 0:sm.n], AF.Silu, R_ps(bG), [("sgb", gb, s)])
                    tt("dve", hid[s][:, hc, 0:sm.n], P[:, bU, 0:sm.n], sv, ALU.mult,
                       R_ps(bU) + [("sgb", gb, s)], [R_hid(hc, s)])
            run_units(sms, f"GU{which}", 11, p1)
            act(tblw[:, 0:1], tblw[:, 1:2], AF.Exp,
                ["tblw"] + [("sgb", gb_, s_) for gb_ in range(2) for s_ in range(2)], ["tblw"])
            bS = {}
            pending = {}

            def p2(oc, slot_w, sm):
                s = sm.s
                if s not in bS:
                    bS[s] = alloc_bank()
                    st["held"].add(bS[s])
                    pending[s] = None
                wd = wv(slot_w, 0, HC, 128)
                bY = alloc_bank()
                mm_group(bY, sm.n, [wd[:, k, :] for k in range(HC)],
                         [hid[s][:, k, 0:sm.n] for k in range(HC)],
                         [R_hid(k, s) for k in range(HC)] + [("w", slot_w)])
                if pending[s] is not None:
                    pending[s]()
                pending[s] = evac_with_stats(bY, bS[s], sm, oc, y32, R_y32, dcol(ghalf, oc),
                                             oc == 0, oc == KC - 1)

            def f2(sm):
                pending[sm.s]()
                post_norm_residual(sm, bS[sm.s], y32, R_y32)
                if next_gvi is not None:
                    norm_to_h(sm, next_gvi)
            run_units(sms, f"DN{which}", KC, p2, f2)

        def mixer(ti, sms, nseq, T, next_gvi):
            last_tile = (ti == len(TILES) - 1)

            def sigmoid_act(Bx, nm, q, sm, bank, bias_col):
                bv = Bx[:, q, sm.c0:sm.c1]
                key = (nm, q, sm.s)
                if bias_col is None:
                    act(bv, P[:, bank, 0:sm.n], AF.Exp, R_ps(bank), [key], scale=-1.0)
                else:
                    act(bv, P[:, bank, 0:sm.n], AF.Exp, R_ps(bank) + DP_ALL, [key], scale=-1.0, bias=bias_col)
                act(bv, bv, AF.Ln, [key] + DP_ALL, [key], bias=1.0)
                act(bv, bv, AF.Exp, [key], [key], scale=-1.0)

            def stage1(j, sm, slot_w):
                s = sm.s
                q = j % NSET
                q3 = j % 2
                c0, c1, n = sm.c0, sm.c1, sm.n
                q0, q1 = sm.q0, sm.q1
                w4 = wring[:, slot_w, 0:4096].rearrange("p (s k c) -> p s k c", s=4, k=KC)
                rhs = [hbf[s][:, k, 0:n] for k in range(KC)]
                rd = [R_h(k, s) for k in range(KC)] + [("w", slot_w)]

                def grp(sidx):
                    bk = alloc_bank()
                    mm_group(bk, n, [w4[:, sidx, k, :] for k in range(KC)], rhs, rd)
                    return bk
                b_bx = grp(3)
                b_ac = grp(1)
                b_ax = grp(2)
                b_ab = grp(0)
                if s == 0:
                    cp("dve", acx[:, q, 0:2], CAR[:, j, 0:2], [("CAR", j)], [("acx_h", q)])
                    cp("dve", bxb[:, q, 0:3], CAR[:, j, 2:5], [("CAR", j)], [("bx_h", q)])
                    halo_a = [("acx_h", q)]
                    halo_b = [("bx_h", q)]
                else:
                    halo_a = [("acx", q, 0)]
                    halo_b = [("bx", q, 0)]
                act(bxb[:, q, 3 + c0:3 + c1], P[:, b_bx, 0:n], AF.Identity, R_ps(b_bx), [("bx", q, s)])
                act(acx[:, q, 2 + c0:2 + c1], P[:, b_ac, 0:n], AF.Identity, R_ps(b_ac), [("acx", q, s)])
                cbk = ("cb", q3, s)
                ts("dve", cbb[:, q3, q0:q1], bxb[:, q, q0:q1], pcol(9, j), pcol(13, j), ALU.mult, ALU.add,
                   [("bx", q, s)] + halo_b + DP_ALL, [cbk])
                tt("dve", acx[:, q, 2 + c0:2 + c1], P[:, b_ax, 0:n], acx[:, q, 2 + c0:2 + c1], ALU.mult,
                   R_ps(b_ax) + [("acx", q, s)], [("acx", q, s)])
                for k in (1, 2, 3):
                    stt("dve", cbb[:, q3, q0:q1], bxb[:, q, k + q0:k + q1], pcol(9 + k, j), cbb[:, q3, q0:q1],
                        ALU.mult, ALU.add, [("bx", q, s), cbk] + halo_b + DP_ALL, [cbk])
                if sm.samp:
                    ts("dve", cbb[:, q3, nseq:T], stT[:, j, 32:48], pcol(9, j), pcol(13, j), ALU.mult, ALU.add,
                       ["stT", cbk] + DP_ALL, [cbk])
                    stt("dve", cbb[:, q3, nseq:T], stT[:, j, 48:64], pcol(10, j), cbb[:, q3, nseq:T],
                        ALU.mult, ALU.add, ["stT", cbk] + DP_ALL, [cbk])
                    stt("dve", cbb[:, q3, nseq:T], stT[:, j, 64:80], pcol(11, j), cbb[:, q3, nseq:T],
                        ALU.mult, ALU.add, ["stT", cbk] + DP_ALL, [cbk])
                    stt("dve", cbb[:, q3, nseq:T], bxb[:, q, 3 + nseq:3 + T], pcol(12, j), cbb[:, q3, nseq:T],
                        ALU.mult, ALU.add, [("bx", q, s), cbk] + DP_ALL, [cbk])
                act(cbh[:, q, c0:c1], cbb[:, q3, c0:c1], AF.Identity, [cbk], [("cbh", q, s)])
                a2k = ("A2", 0, s)
                ts("dve", A2[:, 0, q0:q1], acx[:, q, q0:q1], pcol(6, j), None, ALU.mult, None,
                   [("acx", q, s)] + halo_a + DP_ALL, [a2k])
                stt("dve", A2[:, 0, q0:q1], acx[:, q, 1 + q0:1 + q1], pcol(7, j), A2[:, 0, q0:q1], ALU.mult, ALU.add,
                    [("acx", q, s), a2k] + halo_a + DP_ALL, [a2k])
                stt("dve", A2[:, 0, q0:q1], acx[:, q, 2 + q0:2 + q1], pcol(8, j), A2[:, 0, q0:q1], ALU.mult, ALU.add,
                    [("acx", q, s), a2k] + DP_ALL, [a2k])
                if sm.samp:
                    ts("dve", A2[:, 0, nseq:T], stT[:, j, 0:16], pcol(6, j), None, ALU.mult, None,
                       ["stT", a2k] + DP_ALL, [a2k])
                    stt("dve", A2[:, 0, nseq:T], stT[:, j, 16:32], pcol(7, j), A2[:, 0, nseq:T], ALU.mult, ALU.add,
                        ["stT", a2k] + DP_ALL, [a2k])
                    stt("dve", A2[:, 0, nseq:T], acx[:, q, 2 + nseq:2 + T], pcol(8, j), A2[:, 0, nseq:T],
                        ALU.mult, ALU.add, [("acx", q, s), a2k] + DP_ALL, [a2k])
                tt("dve", hid[s][:, j, 0:n], P[:, b_ab, 0:n], A2[:, 0, c0:c1], ALU.mult,
                   R_ps(b_ab) + [a2k], [R_ya(j, s)])
                if sm.has_end:
                    if not last_tile:
                        cp("dve", CAR[:, j, 0:2], acx[:, q, nseq:nseq + 2], [("acx", q, s)], [("CAR", j)])
                        cp("dve", CAR[:, j, 2:5], bxb[:, q, nseq:nseq + 3], [("bx", q, s)], [("CAR", j)])
                    else:
                        cp("dve", SO[:, j, 16:32], acx[:, q, 2 + nseq:2 + T], [("acx", q, s)], [("SO", j)])
                        cp("dve", SO[:, j, 64:80], bxb[:, q, 3 + nseq:3 + T], [("bx", q, s)], [("SO", j)])
                        cp("dve", SO[:, j, 96:98], acx[:, q, nseq:nseq + 2], [("acx", q, s)], [("SO", j)])
                        cp("dve", SO[:, j, 98:101], bxb[:, q, nseq:nseq + 3], [("bx", q, s)], [("SO", j)])

            def stage2(j):
                q = j % NSET
                q3 = j % 2
                allk = lambda nm, qq=q: [(nm, qq, 0), (nm, qq, 1)]
                K1, K2, K3 = allk("B1"), allk("B2"), allk("B3")
                b1a, b2a, b3a = B1[:, q, 0:T], B2[:, q, 0:T], B3[:, q, 0:T]
                for sm in sms:
                    s = sm.s
                    c0, c1, n = sm.c0, sm.c1, sm.n
                    rhs = [cbh[:, q, c0:c1]]
                    b_zr = alloc_bank()
                    mm_group(b_zr, n, [wrgs[:, j * 128:(j + 1) * 128]], rhs, [("cbh", q, s), "wrgs"])
                    b_zi = alloc_bank()
                    mm_group(b_zi, n, [wrgs[:, 1024 + j * 128:1024 + (j + 1) * 128]], rhs, [("cbh", q, s), "wrgs"])
                    act(B1[:, q, c0:c1], P[:, b_zr, 0:n], AF.Exp, R_ps(b_zr) + DP_ALL, [("B1", q, s)],
                        scale=-1.0, bias=dcol(0, j))
                    act(B2[:, q, c0:c1], P[:, b_zi, 0:n], AF.Exp, R_ps(b_zi) + DP_ALL, [("B2", q, s)],
                        scale=-1.0, bias=dcol(1, j))
                act(b1a, b1a, AF.Ln, K1 + DP_ALL, K1, bias=1.0)
                act(b1a, b1a, AF.Exp, K1, K1, scale=-1.0)
                act(b3a, b1a, AF.Exp, K1 + DP_ALL, K3, scale=dcol(2, j))
                act(b1a, b1a, AF.Exp, K1 + DP_ALL, K1, scale=dcol(5, j))
                act(b1a, b1a, AF.Ln, K1 + DP_ALL, K1, bias=1.0, scale=-1.0)
                act(b2a, b2a, AF.Ln, K2 + DP_ALL, K2, bias=1.0)
                stt("dve", b1a, b1a, 0.5, b2a, ALU.mult, ALU.subtract, K1 + K2, K1)
                act(b1a, b1a, AF.Exp, K1, K1)
                tt("dve", b2a, b1a, cbb[:, q3, 0:T], ALU.mult, K1 + K2 + [("cb", q3, 0), ("cb", q3, 1)], K2)
                scan("dve", B1[:, q, 0:nseq], B3[:, q, 0:nseq], B2[:, q, 0:nseq], CAR[:, j, 5:6],
                     K3 + K2 + K1 + [("CAR", j)], K1)
                if T > nseq:
                    tt("dve", B1[:, q, nseq:T], B3[:, q, nseq:T], stT[:, j, 80:96], ALU.mult, K3 + ["stT"] + K1, K1)
                    tt("dve", B1[:, q, nseq:T], B1[:, q, nseq:T], B2[:, q, nseq:T], ALU.add, K2 + K1, K1)
                for sm in sms:
                    cp("act", hid[sm.s][:, 8 + j, 0:sm.n], B1[:, q, sm.c0:sm.c1], K1, [R_yb(j, sm.s)])
                if not last_tile:
                    cp("dve", CAR[:, j, 5:6], B1[:, q, nseq - 1:nseq], K1, [("CAR", j)])
                else:
                    cp("dve", SO[:, j, 80:96], B1[:, q, nseq:T], K1, [("SO", j)])
                    cp("dve", SO[:, j, 101:102], B1[:, q, nseq - 1:nseq], K1, [("SO", j)])

            def wa(j, slot_w, sm):
                stage1(j, sm, slot_w)
                if sm.s == 1:
                    stage2(j)
            run_units(sms, "MA", KC, wa, s1_first=True)
            def wg2(u, slot_w, sm):
                s = sm.s
                wgt = wv(slot_w, 0, KC, 512)
                for bb in range(4):
                    j = 4 * u + bb
                    bk = alloc_bank()
                    mm_group(bk, sm.n, [wgt[:, k, bb * 128:(bb + 1) * 128] for k in range(KC)],
                             [hbf[s][:, k, 0:sm.n] for k in range(KC)],
                             [R_h(k, s) for k in range(KC)] + [("w", slot_w)])
                    gb = j % 2
                    sv = sgb[:, gb, sm.c0:sm.c1]
                    act(sv, P[:, bk, 0:sm.n], AF.Gelu_apprx_tanh, R_ps(bk), [("sgb", gb, s)])
                    ybv = hid[s][:, 8 + j, 0:sm.n]
                    tt("dve", ybv, ybv, sv, ALU.mult, [R_yb(j, s), ("sgb", gb, s)], [R_yb(j, s)])
            run_units(sms, "MG", 2, wg2)
            def wb(oc, slot_w, sm):
                w4 = wring[:, slot_w, 0:4096].rearrange("p (s k c) -> p s k c", s=4, k=KC)
                q = oc % NSET
                s = sm.s
                n = sm.n
                hr = [hbf[s][:, k, 0:n] for k in range(KC)]
                hrd = [R_h(k, s) for k in range(KC)] + [("w", slot_w)]
                b_ga = alloc_bank()
                mm_group(b_ga, n, [w4[:, 2, k, :] for k in range(KC)], hr, hrd)
                b_gb = alloc_bank()
                mm_group(b_gb, n, [w4[:, 3, k, :] for k in range(KC)], hr, hrd)
                sigmoid_act(B1, "B1", q, sm, b_ga, None)
                sigmoid_act(B2, "B2", q, sm, b_gb, None)
                b_ya = alloc_bank()
                mm_group(b_ya, n, [w4[:, 0, k, :] for k in range(KC)],
                         [hid[s][:, k, 0:n] for k in range(KC)],
                         [R_ya(k, s) for k in range(KC)] + [("w", slot_w)])
                b_yb = alloc_bank()
                mm_group(b_yb, n, [w4[:, 1, k, :] for k in range(KC)],
                         [hid[s][:, 8 + k, 0:n] for k in range(KC)],
                         [R_yb(k, s) for k in range(KC)] + [("w", slot_w)])
                k1, k2 = ("B1", q, s), ("B2", q, s)
                b1v, b2v = B1[:, q, sm.c0:sm.c1], B2[:, q, sm.c0:sm.c1]
                tt("dve", b1v, P[:, b_ya, 0:n], b1v, ALU.mult, R_ps(b_ya) + [k1], [k1])
                tt("dve", b2v, P[:, b_yb, 0:n], b2v, ALU.mult, R_ps(b_yb) + [k2], [k2])
                tt("dve", mgbf[s][:, oc, 0:n], b1v, b2v, ALU.add, [k1, k2], [R_mg(oc, s)])
            run_units(sms, "MO", KC, wb, s1_first=True)
            bS = {}
            pending = {}

            def wc(u, slot_w, sm):
                s = sm.s
                if s not in bS:
                    bS[s] = alloc_bank()
                    st["held"].add(bS[s])
                    pending[s] = None
                wo = wv(slot_w, 0, KC, 512)
                for bb in range(4):
                    oc = 4 * u + bb
                    bY = alloc_bank()
                    mm_group(bY, sm.n, [wo[:, k, bb * 128:(bb + 1) * 128] for k in range(KC)],
                             [mgbf[s][:, k, 0:sm.n] for k in range(KC)],
                             [R_mg(k, s) for k in range(KC)] + [("w", slot_w)])
                    if pending[s] is not None:
                        pending[s]()
                    pending[s] = evac_with_stats(bY, bS[s], sm, oc, m32, R_m32, pcol(3, oc),
                                                 oc == 0, oc == KC - 1)

            def fc(sm):
                pending[sm.s]()
                post_norm_residual(sm, bS[sm.s], m32, R_m32)
                if next_gvi is not None:
                    norm_to_h(sm, next_gvi)
            run_units(sms, "WO", 2, wc, fc)

        def xs_keys(c0, nr, ns, kcs):
            ss = set()
            if c0 < ns:
                ss.add(0)
            if c0 + nr > ns:
                ss.add(1)
            return [R_xs(k, s) for k in kcs for s in ss]

        def load_tile(row0, T, ns, extra_slots=False):
            nblk = (T + 127) // 128
            for blk in range(nblk):
                c0 = blk * 128
                nr = min(128, T - c0)
                b = load_rows(xin[row0 + c0:row0 + c0 + nr, :], nr, extra_slots)
                if row0 == 0 and blk == 1:
                    st["pace0"] = last_pe_idx()
                if row0 == 0 and blk == 3:
                    st["pace"] = last_pe_idx()
                transpose_in(b, nr,
                             lambda half, c0=c0, nr=nr: cur["xs"][:, 4 * half:4 * half + 4, c0:c0 + nr],
                             lambda half, c0=c0, nr=nr: xs_keys(c0, nr, ns, range(4 * half, 4 * half + 4)))

        def store_tile(row0, T, ns, extra_slots=False):
            nblk = (T + 127) // 128
            ev = []
            for blk in range(nblk):
                c0 = blk * 128
                nr = min(128, T - c0)
                ev.append(store_rows(lambda kc, c0=c0, nr=nr: cur["xs"][:, kc, c0:c0 + nr],
                                     xs_keys(c0, nr, ns, range(KC)), nr,
                                     yout[row0 + c0:row0 + c0 + nr, :], extra_slots))
            return ev

        out_events = []

        def set_buf(ti):
            cur["xb"] = ti % 2
            cur["xs"] = XS[ti % 2]

        def do_load(ti):
            row0, nseq, nsamp = TILES[ti]
            T = nseq + nsamp
            saved = cur["xb"]
            set_buf(ti)
            load_tile(row0, T, NSMAX, extra_slots=(ti == 0))
            io["stg"] = 0
            for sm in [Stm(s, NSMAX, nseq, T) for s in range(2)]:
                norm_stats(sm, rbuf_alt(sm))
            set_buf(saved)

        st["pace"] = None
        do_load(0)
        issue_unit(0, paced=False, dep=st.get("pace0"))
        for g in range(1, NSLOT - 1 - LEAD):
            issue_unit(g)
        for ti, (row0, nseq, nsamp) in enumerate(TILES):
            T = nseq + nsamp
            ns = NSMAX
            set_buf(ti)
            sms = [Stm(s, ns, nseq, T) for s in range(2)]
            for sm in sms:
                norm_apply(sm, 0, rbuf_alt(sm))
                if ti == 0 and sm.s == 0:
                    st["pace"] = len(S.ops) - 1
            ffn(sms, 1, 2)
            last = (ti == len(TILES) - 1)
            if ti == 0:
                S.dma("pool", "c2", lambda e: e.dma_start(out=wrgs[:, :], in_=wrg_d[:, :]), writes=["wrgs"])
                load_states()
            mixer(ti, sms, nseq, T, 4)
            if last:
                out_events.append(store_rows(lambda kc: SO[:, kc, 0:NST_OUT],
                                             [("SO", j) for j in range(KC)] + ["SO_a", "SO_b"], NST_OUT,
                                             stout[0:NST_OUT, :]))
            else:
                do_load(ti + 1)
            ffn(sms, 2, None)
            out_events += store_tile(row0, T, ns, extra_slots=last)
        assert st["unit"] == NUNITS * len(TILES)
        S.wait_all("sp", out_events)

        prog = S.finalize()
        _CACHE['makespan'] = S.makespan

        def replay(engname, e):
            for waits, fn, inc in prog[engname]:
                for k, v in waits:
                    e.wait_ge(sems[k], v)
                if fn is None:
                    continue
                ins = fn(e)
                ins.then_inc(sems[inc[0]], inc[1])

        with nc.Block() as block:
            @block.sync
            def _(e):
                replay("sp", e)

            @block.scalar
            def _(e):
                replay("act", e)

            @block.vector
            def _(e):
                replay("dve", e)

            @block.gpsimd
            def _(e):
                replay("pool", e)

            @block.tensor
            def _(e):
                replay("pe", e)
    return nc


_CACHE = {}


def _get_program():
    if "nc" not in _CACHE:
        _CACHE["nc"] = build_program()
    return _CACHE["nc"]


def kernel(**inputs):
    f = {k: np.asarray(v) for k, v in inputs.items()}
    w = {k: np.ascontiguousarray(f[k][0], dtype=np.float32) for k in (
        "w_ffn1_gate", "w_ffn1_up", "w_ffn1_down", "w_in", "w_out_a", "w_out_b", "w_o",
        "w_ffn2_gate", "w_ffn2_up", "w_ffn2_down")}
    wall = _build_wall(w)
    wrg = np.zeros((128, 2, KC, 128), np.float32)
    for gi, name in enumerate(("w_rg_r", "w_rg_i")):
        wr = f[name][0].astype(np.float32)
        for c in range(KC):
            wrg[0:64, gi, c, 0:64] = wr[2 * c]
            wrg[64:128, gi, c, 64:128] = wr[2 * c + 1]
    wrg = wrg.reshape(128, 2048)
    vecs = [f["g_ffn1_pre"][0], f["g_ffn1_post"][0], f["g_mix_pre"][0], f["g_mix_post"][0],
            f["g_ffn2_pre"][0], f["g_ffn2_post"][0],
            f["conv_a_w"][0, 0], f["conv_a_w"][0, 1], f["conv_a_w"][0, 2],
            f["conv_b_w"][0, 0], f["conv_b_w"][0, 1], f["conv_b_w"][0, 2], f["conv_b_w"][0, 3],
            f["conv_b_b"][0], f["b_rg_r"][0], f["b_rg_i"][0], f["rg_lambda"][0]]
    par = np.stack([np.asarray(v, np.float32).reshape(KC, 128).T for v in vecs], axis=1)
    par = np.ascontiguousarray(par.reshape(128, NPV * 8))
    ident = np.eye(128, dtype=np.float32)
    meta = f["meta_tokens"].astype(np.float32)
    xp = f["x_prompt"].astype(np.float32)
    xsmp = f["x_sample"].astype(np.float32)[:, 0, :]
    sca = f["state_conv_a"][0].astype(np.float32)
    scb = f["state_conv_b"][0].astype(np.float32)
    srg = f["state_rglru"][0].astype(np.float32)
    in_maps = []
    for c in range(NCORES):
        sl = slice(NSAMP * c, NSAMP * (c + 1))
        xin = np.concatenate([meta, xp[c], xsmp[sl]], axis=0)
        stin = np.concatenate([sca[sl, 0], sca[sl, 1], scb[sl, 0], scb[sl, 1], scb[sl, 2], srg[sl]], axis=0)
        in_maps.append({"xin": np.ascontiguousarray(xin), "stin": np.ascontiguousarray(stin),
                        "par": par, "wall": wall, "wrg": wrg, "ident": ident})
    nc = _get_program()
    res = run_bass_kernel_spmd(nc, in_maps, core_ids=list(range(NCORES)))
    B = xp.shape[0]
    y_prompt = np.empty((B, SEQ, D), np.float32)
    y_sample = np.empty((NSAMP * NCORES, 1, D), np.float32)
    nca_p = np.empty((1, B, 2, D), np.float32)
    ncb_p = np.empty((1, B, 3, D), np.float32)
    nh_p = np.empty((1, B, D), np.float32)
    nca_s = np.empty((1, NSAMP * NCORES, 2, D), np.float32)
    ncb_s = np.empty((1, NSAMP * NCORES, 3, D), np.float32)
    nh_s = np.empty((1, NSAMP * NCORES, D), np.float32)
    for c in range(NCORES):
        r = res.results[c]
        yo = r["yout"]
        so = r["stout"]
        sl = slice(NSAMP * c, NSAMP * (c + 1))
        y_prompt[c] = yo[NMETA:NSEQ]
        y_sample[sl, 0] = yo[NSEQ:NTOK]
        nca_s[0, sl, 0] = so[0:16]
        nca_s[0, sl, 1] = so[16:32]
        ncb_s[0, sl, 0] = so[32:48]
        ncb_s[0, sl, 1] = so[48:64]
        ncb_s[0, sl, 2] = so[64:80]
        nh_s[0, sl] = so[80:96]
        nca_p[0, c] = so[96:98]
        ncb_p[0, c] = so[98:101]
        nh_p[0, c] = so[101]
    return (y_prompt, y_sample, nca_p, ncb_p, nh_p, nca_s, ncb_s, nh_s)
```
